# Optimizing a Trainium2 kernel written in Bass

```python
import math
import jax, jax.numpy as jnp
from jax import lax
import numpy as np


D_MODEL = 1024
BATCH = 8
SEQ = 2048
DEPTH = 1
DEC_BATCH = 32
DEC_SEQ = 8
PAST_LEN = 16384
PAGE_SIZE = 128

D_LRU = D_MODEL
N_LRU_BLOCKS = 8
LRU_BLOCK = D_LRU // N_LRU_BLOCKS
CONV_WIDTH = 4
LRU_C = 8.0
HEAD_DIM = 128
HEADS_PER_GROUP = 4
WINDOWS = (128, 512, 2048)
DILATIONS = (1, 4, 16)
N_GROUPS = 3
N_ATTN_HEADS = N_GROUPS * HEADS_PER_GROUP
N_BACK = 128
N_SLOTS = N_BACK + 1
BLK = 128
D_QKV = N_ATTN_HEADS * HEAD_DIM
D_ATTN_OUT = HEADS_PER_GROUP * HEAD_DIM
ATTN_SCALE = HEAD_DIM ** -0.5
N_BUCKETS = 32
MAX_DISTANCE = 2048
NORM_EPS = 1e-6
D_IN = 2 * D_LRU + 3 * D_QKV + D_ATTN_OUT + 2 * D_MODEL

kernel_name = 'hybrid_rglru_dilated_swa_decode_step'


def rms_norm(x, g):
    xf = x.astype(jnp.float32)
    y = xf * lax.rsqrt(jnp.mean(xf * xf, axis=-1, keepdims=True) + NORM_EPS)
    return (y * g.astype(jnp.float32)).astype(x.dtype)


def t5_bucket(dist):
    max_exact = N_BUCKETS // 2
    df = jnp.maximum(dist, 1).astype(jnp.float32)
    large = max_exact + (jnp.log(df / max_exact) / math.log(MAX_DISTANCE / max_exact)
                         * (N_BUCKETS - max_exact)).astype(jnp.int32)
    return jnp.where(dist < max_exact, dist, jnp.minimum(large, N_BUCKETS - 1))


def slot_bias(rel_bias, g):
    dist = jnp.arange(N_SLOTS, dtype=jnp.int32) * DILATIONS[g]
    return rel_bias[t5_bucket(dist), g * HEADS_PER_GROUP:(g + 1) * HEADS_PER_GROUP].astype(jnp.float32)


def softmax_stats(s, valid):
    s = jnp.where(valid, s, -jnp.inf)
    m = jnp.max(s, axis=-1, keepdims=True)
    p = jnp.exp(s - m)
    den = jnp.sum(p, axis=-1, keepdims=True)
    return p / den, (m + jnp.log(den))[..., 0]


def dilated_group_prompt(q, k, v, bias, dil):
    B, S, H, HD = q.shape
    span = dil * BLK
    s_pad = -(-S // span) * span
    m_len = s_pad // dil
    n_blk = m_len // BLK

    def classes(t):
        t = jnp.pad(t, ((0, 0), (0, s_pad - S), (0, 0), (0, 0))).reshape(B, m_len, dil, H, HD)
        return jnp.moveaxis(t, 2, 1).reshape(B, dil, n_blk, BLK, H, HD)

    def with_prev(t):
        prev = jnp.pad(t, ((0, 0), (0, 0), (1, 0), (0, 0), (0, 0), (0, 0)))[:, :, :-1]
        return jnp.concatenate([prev, t], axis=3)

    qc = classes(q)
    kb = with_prev(classes(k))
    vb = with_prev(classes(v))
    qi = jnp.arange(BLK)[:, None]
    ki = jnp.arange(2 * BLK)[None, :]
    dist = qi + BLK - ki
    in_band = (dist >= 0) & (dist <= N_BACK)
    has_prev = jnp.arange(n_blk)[:, None, None] > 0
    valid = in_band[None] & (has_prev | (ki >= BLK)[None])
    b = jnp.transpose(bias[jnp.clip(dist, 0, N_BACK)], (2, 0, 1))
    s = jnp.einsum('brnqhe,brnkhe->brnhqk', qc, kb, preferred_element_type=jnp.float32)
    s = s * ATTN_SCALE + b
    p, lse = softmax_stats(s, valid[None, None, :, None])
    o = jnp.einsum('brnhqk,brnkhe->brnqhe', p.astype(vb.dtype), vb)
    o = jnp.moveaxis(o.reshape(B, dil, m_len, H, HD), 1, 2).reshape(B, s_pad, H, HD)[:, :S]
    lse = jnp.moveaxis(jnp.moveaxis(lse, 3, 4).reshape(B, dil, m_len, H), 1, 2).reshape(B, s_pad, H)[:, :S]
    return o, lse


def dilated_group_sample(q, k, v, kv_buf, bias, dil):
    L = kv_buf.shape[1]
    T = q.shape[1]
    kc = jnp.concatenate([kv_buf[:, :, 0], k.astype(kv_buf.dtype)], axis=1)
    vc = jnp.concatenate([kv_buf[:, :, 1], v.astype(kv_buf.dtype)], axis=1)
    idx = L + jnp.arange(T)[:, None] - dil * jnp.arange(N_SLOTS)[None, :]
    valid = idx >= 0
    idx = jnp.maximum(idx, 0)
    kg = kc[:, idx]
    vg = vc[:, idx]
    s = jnp.einsum('bthe,btjhe->bthj', q, kg, preferred_element_type=jnp.float32)
    s = s * ATTN_SCALE + bias.T
    p, lse = softmax_stats(s, valid[None, :, None, :])
    o = jnp.einsum('bthj,btjhe->bthe', p.astype(vg.dtype), vg)
    new_buf = jnp.stack([kc, vc], axis=2)[:, T:]
    return o, lse, new_buf


def causal_conv(x, buf, w, b):
    T = x.shape[1]
    xc = jnp.concatenate([buf.astype(x.dtype), x], axis=1)
    y = b + sum(xc[:, j:j + T] * w[j] for j in range(CONV_WIDTH))
    return y, xc[:, T:]


def rg_lru(x, h0, w_a, b_a, w_x, b_x, lam):
    B, T, _ = x.shape
    f32 = jnp.float32
    xf = x.astype(f32)
    xb = xf.reshape(B, T, N_LRU_BLOCKS, LRU_BLOCK)
    gate_a = jnp.einsum('btni,nij->btnj', xb, w_a.astype(f32)).reshape(B, T, D_LRU) + b_a.astype(f32)
    gate_x = jnp.einsum('btni,nij->btnj', xb, w_x.astype(f32)).reshape(B, T, D_LRU) + b_x.astype(f32)
    r = jax.nn.sigmoid(gate_a)
    i = jax.nn.sigmoid(gate_x)
    log_a = -LRU_C * r * jax.nn.softplus(-lam.astype(f32))
    a = jnp.exp(log_a)
    u = jnp.sqrt(-jnp.expm1(2.0 * log_a)) * (i * xf)

    def step(h, au):
        a_t, u_t = au
        h = a_t * h + u_t
        return h, h

    h_last, hs = lax.scan(step, h0.astype(f32), (jnp.moveaxis(a, 1, 0), jnp.moveaxis(u, 1, 0)))
    return jnp.moveaxis(hs, 0, 1), h_last


def mixer_layer(x, conv_buf, h0, kv_bufs, g_norm, w_in, b_merge, conv_w, conv_b,
                lru_w_a, lru_b_a, lru_w_x, lru_b_x, lru_lambda, g_q, g_k, rel_bias,
                w_lru_proj, w_attn_proj, w_out):
    B, T, _ = x.shape
    f32 = jnp.float32
    u = rms_norm(x, g_norm)
    z = jnp.einsum('btd,de->bte', u, w_in)
    o1 = D_LRU
    o2 = 2 * D_LRU
    o3 = o2 + D_QKV
    o4 = o3 + D_QKV
    o5 = o4 + D_QKV
    o6 = o5 + D_ATTN_OUT
    xa, ga, q, k, v, gb, gm = jnp.split(z, [o1, o2, o3, o4, o5, o6], axis=-1)

    xa, new_conv = causal_conv(xa, conv_buf, conv_w, conv_b)
    hs, h_last = rg_lru(xa, h0, lru_w_a, lru_b_a, lru_w_x, lru_b_x, lru_lambda)
    ya = jnp.einsum('bte,ed->btd', (hs * jax.nn.silu(ga.astype(f32))).astype(x.dtype), w_lru_proj)

    q = rms_norm(q.reshape(B, T, N_ATTN_HEADS, HEAD_DIM), g_q)
    k = rms_norm(k.reshape(B, T, N_ATTN_HEADS, HEAD_DIM), g_k)
    v = v.reshape(B, T, N_ATTN_HEADS, HEAD_DIM)
    outs, lses, new_bufs = [], [], []
    for g in range(N_GROUPS):
        sl = slice(g * HEADS_PER_GROUP, (g + 1) * HEADS_PER_GROUP)
        bias = slot_bias(rel_bias, g)
        qg, kg, vg = q[:, :, sl], k[:, :, sl], v[:, :, sl]
        if kv_bufs is None:
            o, lse = dilated_group_prompt(qg, kg, vg, bias, DILATIONS[g])
            keep = min(WINDOWS[g], T)
            nb = jnp.stack([kg, vg], axis=2)[:, T - keep:]
        else:
            o, lse, nb = dilated_group_sample(qg, kg, vg, kv_bufs[g], bias, DILATIONS[g])
        outs.append(o.astype(f32))
        lses.append(lse)
        new_bufs.append(nb)
    alpha = jax.nn.softmax(jnp.stack(lses, axis=0), axis=0)
    ob = jnp.sum(alpha[..., None] * jnp.stack(outs, axis=0), axis=0).reshape(B, T, D_ATTN_OUT)
    yb = jnp.einsum('bte,ed->btd', (ob * jax.nn.silu(gb.astype(f32))).astype(x.dtype), w_attn_proj)

    gates = jax.nn.sigmoid((gm + b_merge).astype(f32))
    merged = gates[..., :D_MODEL] * ya.astype(f32) + gates[..., D_MODEL:] * yb.astype(f32)
    y = x + jnp.einsum('btd,de->bte', merged.astype(x.dtype), w_out)
    return y, new_conv, h_last.astype(x.dtype), new_bufs


def setup_inputs(seed: int = 0) -> dict:
    key = jax.random.key(seed)
    ks = jax.random.split(key, 24)

    def nrm(k, shape, scale):
        return scale * jax.random.normal(k, shape, jnp.float32)

    kv_shape = lambda w: (DEPTH, DEC_BATCH, min(w, PAST_LEN), 2, HEADS_PER_GROUP, HEAD_DIM)
    u_lam = jax.random.uniform(ks[16], (DEPTH, D_LRU), jnp.float32, minval=0.9, maxval=0.999)
    s_lam = u_lam ** (1.0 / LRU_C)
    lru_lambda = jnp.log(s_lam) - jnp.log1p(-s_lam)
    return {
        'x_prompt': nrm(ks[0], (BATCH, SEQ, D_MODEL), 1.0),
        'x_sample': nrm(ks[1], (DEC_BATCH, DEC_SEQ, D_MODEL), 1.0),
        'cache_kv_w128': nrm(ks[2], kv_shape(WINDOWS[0]), 1.0),
        'cache_kv_w512': nrm(ks[3], kv_shape(WINDOWS[1]), 1.0),
        'cache_kv_w2048': nrm(ks[4], kv_shape(WINDOWS[2]), 1.0),
        'state_conv': nrm(ks[5], (DEPTH, DEC_BATCH, CONV_WIDTH - 1, D_LRU), 1.0),
        'state_h': nrm(ks[6], (DEPTH, DEC_BATCH, D_LRU), 0.5),
        'g_norm': 1.0 + nrm(ks[7], (DEPTH, D_MODEL), 0.05),
        'w_in': nrm(ks[8], (DEPTH, D_MODEL, D_IN), D_MODEL ** -0.5),
        'b_merge': nrm(ks[9], (DEPTH, 2 * D_MODEL), 0.1),
        'conv_w': nrm(ks[10], (DEPTH, CONV_WIDTH, D_LRU), CONV_WIDTH ** -0.5),
        'conv_b': nrm(ks[11], (DEPTH, D_LRU), 0.02),
        'lru_w_a': nrm(ks[12], (DEPTH, N_LRU_BLOCKS, LRU_BLOCK, LRU_BLOCK), LRU_BLOCK ** -0.5),
        'lru_b_a': nrm(ks[13], (DEPTH, D_LRU), 0.1),
        'lru_w_x': nrm(ks[14], (DEPTH, N_LRU_BLOCKS, LRU_BLOCK, LRU_BLOCK), LRU_BLOCK ** -0.5),
        'lru_b_x': nrm(ks[15], (DEPTH, D_LRU), 0.1),
        'lru_lambda': lru_lambda,
        'g_q': 1.0 + nrm(ks[17], (DEPTH, HEAD_DIM), 0.05),
        'g_k': 1.0 + nrm(ks[18], (DEPTH, HEAD_DIM), 0.05),
        'rel_bias': nrm(ks[19], (N_BUCKETS, N_ATTN_HEADS), 0.5),
        'w_lru_proj': nrm(ks[20], (DEPTH, D_LRU, D_MODEL), D_LRU ** -0.5),
        'w_attn_proj': nrm(ks[21], (DEPTH, D_ATTN_OUT, D_MODEL), D_ATTN_OUT ** -0.5),
        'w_out': nrm(ks[22], (DEPTH, D_MODEL, D_MODEL), D_MODEL ** -0.5),
    }


def reference(x_prompt, x_sample, cache_kv_w128, cache_kv_w512, cache_kv_w2048, state_conv, state_h,
              g_norm, w_in, b_merge, conv_w, conv_b, lru_w_a, lru_b_a, lru_w_x, lru_b_x, lru_lambda,
              g_q, g_k, rel_bias, w_lru_proj, w_attn_proj, w_out):
    yp, ys = x_prompt, x_sample
    bp = x_prompt.shape[0]
    kv_p = [[], [], []]
    kv_s = [[], [], []]
    conv_p, h_p, conv_s, h_s = [], [], [], []
    for l in range(DEPTH):
        params = (g_norm[l], w_in[l], b_merge[l], conv_w[l], conv_b[l], lru_w_a[l], lru_b_a[l],
                  lru_w_x[l], lru_b_x[l], lru_lambda[l], g_q[l], g_k[l], rel_bias,
                  w_lru_proj[l], w_attn_proj[l], w_out[l])
        yp, cp, hp, bufs_p = mixer_layer(
            yp, jnp.zeros((bp, CONV_WIDTH - 1, D_LRU), yp.dtype), jnp.zeros((bp, D_LRU), jnp.float32),
            None, *params)
        ys, cs, hsm, bufs_s = mixer_layer(
            ys, state_conv[l], state_h[l], (cache_kv_w128[l], cache_kv_w512[l], cache_kv_w2048[l]), *params)
        for g in range(N_GROUPS):
            kv_p[g].append(bufs_p[g])
            kv_s[g].append(bufs_s[g])
        conv_p.append(cp)
        h_p.append(hp)
        conv_s.append(cs)
        h_s.append(hsm)
    kv128_p, kv512_p, kv2048_p = (jnp.stack(t, axis=0) for t in kv_p)
    kv128_s, kv512_s, kv2048_s = (jnp.stack(t, axis=0) for t in kv_s)
    conv_prompt = jnp.stack(conv_p, axis=0)
    h_prompt = jnp.stack(h_p, axis=0)
    conv_sample = jnp.stack(conv_s, axis=0)
    h_sample = jnp.stack(h_s, axis=0)
    return (yp, ys, kv128_p, kv512_p, kv2048_p, conv_prompt, h_prompt,
            kv128_s, kv512_s, kv2048_s, conv_sample, h_sample)
```

```python
import numpy as np
import concourse.bass as bass
import concourse.mybir as mybir
from concourse.bass_utils import run_bass_kernel_spmd

F32 = mybir.dt.float32
BF16 = mybir.dt.bfloat16
AF = mybir.ActivationFunctionType
ALU = mybir.AluOpType
AX = mybir.AxisListType


class Ins:
    __slots__ = ("eng", "emit", "deps", "tick", "is_dma", "key", "sig", "waits", "seq")


class Prog:
    COMPUTE = ("pe", "act", "dve", "pool")

    def __init__(self, nc):
        self.nc = nc
        self.order = []
        self.lastw = {}
        self.readers = {}
        self.dma_cum = {}
        self.out_keys = set()
        self.last_c = {}
        self.last_d = {}
        self.pending = {}
        self.nosbuf_keys = set()

    def barrier(self):
        snap = list(self.last_c.values()) + [v for k, v in self.last_d.items() if k not in self.nosbuf_keys]
        self.pending = {e: snap for e in ("pe", "act", "dve", "pool", "sp")}

    def add(self, eng, emit, reads=(), writes=(), dma_key=None, out=False):
        ins = Ins()
        ins.eng = eng
        ins.emit = emit
        ins.is_dma = dma_key is not None
        ins.key = dma_key
        ins.sig = False
        ins.tick = 0
        d = {}
        for r in reads:
            w = self.lastw.get(r)
            if w is not None:
                d[w] = "RAW"
            if isinstance(r, tuple) and r[0] == "ps":
                for rd in self.readers.get(r, ()):
                    if rd.eng != eng and rd not in d:
                        d[rd] = "WAR"
        for r in writes:
            w = self.lastw.get(r)
            if w is not None and w not in d:
                d[w] = "WAW"
            for rd in self.readers.get(r, ()):
                if rd not in d:
                    d[rd] = "WAR"
        for r in reads:
            self.readers.setdefault(r, []).append(ins)
        for r in writes:
            self.lastw[r] = ins
            self.readers[r] = []
        deps = []
        for dep, kind in d.items():
            if dep is ins:
                continue
            if (not dep.is_dma) and (not ins.is_dma) and dep.eng == ins.eng:
                if ins.eng == "pe":
                    continue
                if kind == "WAR":
                    continue
            deps.append(dep)
        if ins.eng in self.pending:
            for x in self.pending.pop(ins.eng):
                if x.eng == ins.eng and (not x.is_dma) and (not ins.is_dma):
                    continue
                if x not in d:
                    deps.append(x)
        best = {}
        rest = []
        for dep in deps:
            if dep.is_dma:
                rest.append(dep)
            else:
                b = best.get(dep.eng)
                if b is None or dep.seq > b.seq:
                    best[dep.eng] = dep
        ins.deps = rest + list(best.values())
        ins.seq = len(self.order)
        if ins.is_dma:
            self.last_d[dma_key] = ins
        else:
            self.last_c[ins.eng] = ins
        if ins.is_dma:
            self.dma_cum[dma_key] = self.dma_cum.get(dma_key, 0) + 16
            ins.tick = self.dma_cum[dma_key]
            if out:
                self.out_keys.add(dma_key)
        self.order.append(ins)
        return ins

    def finalize(self):
        for ins in self.order:
            for dep in ins.deps:
                if not dep.is_dma:
                    dep.sig = True
        cnt = {e: 0 for e in self.COMPUTE}
        for ins in self.order:
            if (not ins.is_dma) and ins.sig:
                cnt[ins.eng] += 1
                ins.tick = cnt[ins.eng]
        cum = {}
        waited = {}
        per_eng = {}
        for ins in self.order:
            w = {}
            for dep in ins.deps:
                if dep.is_dma:
                    k = ("dma", dep.key)
                    v = cum[dep.key]
                else:
                    k = ("eng", dep.eng)
                    v = dep.tick
                if v > w.get(k, 0):
                    w[k] = v
            if ins.is_dma:
                cum[ins.key] = ins.tick
            wl = []
            we = waited.setdefault(ins.eng, {})
            for k, v in w.items():
                if we.get(k, 0) >= v:
                    continue
                we[k] = v
                wl.append((k, v))
            ins.waits = wl
            per_eng.setdefault(ins.eng, []).append(ins)
        self.per_eng = per_eng
        self.counts = cnt

    def run(self, sems_needed=None):
        nc = self.nc
        self.finalize()
        import contextlib

        with contextlib.ExitStack() as st:
            esem = {e: st.enter_context(nc.semaphore("sem_" + e)) for e in self.COMPUTE}
            dsem = {k: st.enter_context(nc.semaphore("dsem_%d" % i)) for i, k in enumerate(self.dma_cum)}
            block = st.enter_context(nc.Block())

            def emit_all(eng_name, eng):
                for ins in self.per_eng.get(eng_name, []):
                    for (k, v) in ins.waits:
                        s = dsem[k[1]] if k[0] == "dma" else esem[k[1]]
                        eng.wait_ge(s, v)
                    r = ins.emit(eng)
                    if ins.is_dma:
                        r.then_inc(dsem[ins.key], 16)
                    elif ins.sig:
                        r.then_inc(esem[ins.eng], 1)
                if eng_name == "sp":
                    for k in self.out_keys:
                        eng.wait_ge(dsem[k], self.dma_cum[k])

            @block.sync
            def _(e):
                emit_all("sp", e)

            @block.tensor
            def _(e):
                emit_all("pe", e)

            @block.scalar
            def _(e):
                emit_all("act", e)

            @block.vector
            def _(e):
                emit_all("dve", e)

            @block.gpsimd
            def _(e):
                emit_all("pool", e)


import contextlib
import math

NT = 2080
NPR = 2048
TCS = [(0, 512), (512, 512), (1024, 512), (1536, 512), (2048, 32)]
EPS = 1e-6
DIL = (1, 4, 16)
WIN = (128, 512, 2048)
NEG = -30000.0
O1, O2, O3, O4, O5, O6 = 1024, 2048, 3584, 5120, 6656, 7168
NPAR = 90
LL = 385
LT = 16


def _t5_bucket(dist):
    dist = np.asarray(dist, np.int64)
    df = np.maximum(dist, 1).astype(np.float32)
    large = 16 + (np.log(df / np.float32(16)) / np.float32(math.log(2048 / 16)) * np.float32(16)).astype(np.int32)
    return np.where(dist < 16, dist, np.minimum(large, 31))


def _cmat():
    C = np.zeros((33, 3, LL + LT), np.float32)
    for g, d in enumerate(DIL):
        for i in range(LL):
            if 128 <= i <= 256:
                C[int(_t5_bucket((i - 128) * d)), g, i] = 1.0
            else:
                C[32, g, i] = NEG
        for i in range(LT):
            dl = i - 8
            if dl >= 0 and dl % d == 0:
                C[int(_t5_bucket(dl)), g, LL + i] = 1.0
            else:
                C[32, g, LL + i] = NEG
    return C


import os
KSTOP = int(os.environ.get("KSTOP", "99"))


def build_program():
    nc = bass.Bass("TRN2", target_bir_lowering=False)
    P = Prog(nc)
    D = {}

    def din(name, shape):
        D[name] = nc.dram_tensor(name, list(shape), F32, kind="ExternalInput")

    def dout(name, shape):
        D[name] = nc.dram_tensor(name, list(shape), F32, kind="ExternalOutput")

    din("xT", (128, 8, NT)); din("xtok", (NT, 1024))
    din("wv", (2, 3, 128, 8, 256)); din("wqk", (4, 7, 128, 8, 128)); din("wlru", (8, 2, 128, 8, 128))
    din("wa", (128, 8, 128)); din("wx", (128, 8, 128)); din("wfin", (8, 128, 28, 128)); din("wout", (128, 8, 1024))
    din("par", (128, NPAR)); din("relb", (33, 12)); din("cmat", (33, 3, LL + LT)); din("ident", (128, 128))
    din("c128", (4, 128, 1024)); din("c512", (4, 512, 1024)); din("c2048", (4, 2048, 1024))
    din("sconv", (128, 8, 4, 3)); din("sh", (128, 8, 4))
    dout("ytok", (NT, 1024)); dout("kvp128", (128, 1024)); dout("kvp512", (512, 1024)); dout("kvp2048", (2048, 1024))
    dout("small", (128, 8, 20))
    dout("kvs128", (4, 128, 1024)); dout("kvs512", (4, 512, 1024)); dout("kvs2048", (4, 2048, 1024))
    LLT = LL + LT
    scr = nc.dram_tensor("scr", [12, 128, LLT], F32, kind="Internal")
    scr1 = nc.dram_tensor("scr1", [12, LLT], F32, kind="Internal")
    caches = [D["c128"], D["c512"], D["c2048"]]
    kvs = [D["kvs128"], D["kvs512"], D["kvs2048"]]
    kvp = [D["kvp128"], D["kvp512"], D["kvp2048"]]

    ST = contextlib.ExitStack()

    def sb(name, shape, dt=F32, stack=None):
        return (stack or ST).enter_context(nc.sbuf_tensor("s_" + name, list(shape), dt))

    ps = [ST.enter_context(nc.psum_tensor("ps%d" % i, [128, 512], F32)) for i in range(8)]
    bank_ctr = [0]

    reserved = set()

    def bank(reserve=False):
        while True:
            b = bank_ctr[0] % 8
            bank_ctr[0] += 1
            if b not in reserved:
                break
        if reserve:
            reserved.add(b)
        return b

    def PB(b):
        return ("ps", b)

    ctr = {}

    def rot(name, n):
        v = ctr.get(name, 0)
        ctr[name] = v + 1
        return v % n

    def mm(out, lhsT, rhs, start, stop, reads, writes):
        P.add("pe", lambda e: e.matmul(out, lhsT=lhsT, rhs=rhs, start=start, stop=stop, skip_group_check=True),
              reads=reads, writes=writes)

    def act(out, in_, func, reads, writes, scale=1.0, bias=0.0):
        P.add("act", lambda e: e.activation(out=out, in_=in_, func=func, scale=scale, bias=bias), reads=reads, writes=writes)

    def dve_tt(out, in0, in1, op, reads, writes):
        P.add("dve", lambda e: e.tensor_tensor(out=out, in0=in0, in1=in1, op=op), reads=reads, writes=writes)

    def dve_stt(out, in0, scalar, in1, op0, op1, reads, writes):
        P.add("dve", lambda e: e.scalar_tensor_tensor(out=out, in0=in0, scalar=scalar, in1=in1, op0=op0, op1=op1),
              reads=reads, writes=writes)

    def dve_ts(out, in0, s1, s2, op0, op1, reads, writes):
        if s2 is None:
            P.add("dve", lambda e: e.tensor_scalar(out=out, in0=in0, scalar1=s1, scalar2=None, op0=op0), reads=reads, writes=writes)
        else:
            P.add("dve", lambda e: e.tensor_scalar(out=out, in0=in0, scalar1=s1, scalar2=s2, op0=op0, op1=op1),
                  reads=reads, writes=writes)

    def dve_cp(out, in_, reads, writes):
        P.add("dve", lambda e: e.tensor_copy(out=out, in_=in_), reads=reads, writes=writes)

    def dve_rcp(out, in_, reads, writes):
        P.add("dve", lambda e: e.reciprocal(out=out, in_=in_), reads=reads, writes=writes)

    def dma(eng, out, in_, reads, writes, key, is_out=False):
        P.add(eng, lambda e: e.dma_start(out=out, in_=in_), reads=reads, writes=writes, dma_key=key, out=is_out)

    def bcast_mid(ap2d, n):
        a = ap2d.ap
        return bass.AP(ap2d.tensor, ap2d.offset, [list(a[0]), [0, n]] + [list(x) for x in a[1:]])

    ident_f = sb("ident_f", (128, 128)); ident_b = sb("ident_b", (128, 128), BF16); ones_b = sb("ones_b", (128, 128), BF16)
    par = sb("par", (128, NPAR)); der = sb("der", (128, 32))
    uT = sb("uT", (128, 8, NT), BF16)
    obg = sb("obg", (128, 4, NT), BF16)
    small = sb("small", (128, 8, 20))
    scv = sb("scv", (128, 8, 4, 3)); shh = sb("shh", (128, 8, 4))
    ring = [sb("ring%d" % i, (128, 8, 128), BF16) for i in range(6)]
    qTs = sb("qTs", (128, 12, 32), BF16); sgbS = sb("sgbS", (128, 4, 32), F32); BTps = sb("BTps", (128, 12, 8), F32)
    KVmK = [sb("KVmK%d" % i, (128, 512), BF16) for i in range(5)]
    KVmV = [sb("KVmV%d" % i, (128, 512), BF16) for i in range(9)]
    kTm = [sb("kTm%d" % i, (128, 512), BF16) for i in range(3)]
    Pfs = [sb("Pfs%d" % i, (128, 32), F32) for i in range(3)]
    PTs = [sb("PTs%d" % i, (128, 32), BF16) for i in range(3)]
    rdens = sb("rdens", (128, 128), F32); otmps = sb("otmps", (128, 128), F32)

    def ring_load(src_ap):
        s = rot("ring", 6)
        dma("pool", ring[s][:], src_ap, [], [("ring", s)], ("ring", s))
        return s

    cc_pieces = []
    for g in range(3):
        L = WIN[g]
        npc = 4 if g == 2 else 1
        rows = (L - 8) // npc
        for b in range(4):
            for pc in range(npc):
                n = rows * 1024
                o_dst = b * L * 1024 + pc * n
                o_src = b * L * 1024 + 8 * 1024 + pc * n
                cc_pieces.append((g, b, o_dst, o_src, n))
    P.nosbuf_keys.add("ccopy")

    def cc_next():
        if not cc_pieces:
            return
        g, b, o_dst, o_src, n = cc_pieces.pop()
        dst = bass.AP(kvs[g], o_dst, [[n // 16, 16], [1, n // 16]])
        src = bass.AP(caches[g], o_src, [[n // 16, 16], [1, n // 16]])
        dma("sp", dst, src, [], [("kvs", g, b, o_dst)], "ccopy", is_out=True)

    dma("sp", ident_f[:], D["ident"].ap(), [], ["ident_f"], "c0")
    dma("sp", par[:], D["par"].ap(), [], ["par"], "c1")
    dma("sp", scv[:], D["sconv"].ap(), [], ["scv"], "c2")
    dma("sp", shh[:], D["sh"].ap(), [], ["shh"], "c3")
    dve_cp(ident_b[:], ident_f[:], ["ident_f"], ["ident_b"])
    P.add("dve", lambda e: e.memset(ones_b[:], 1.0), writes=["ones_b"])
    dve_ts(der[:, 0:1], par[:, 88:89], float(128 ** -0.5), None, ALU.mult, None, ["par"], [("der", 0)])
    act(der[:, 17:25], par[:, 64:72], AF.Exp, ["par"], [("der", 2)], scale=-1.0)
    act(der[:, 17:25], der[:, 17:25], AF.Ln, [("der", 2)], [("der", 2)], bias=1.0)
    dve_ts(der[:, 1:9], der[:, 17:25], -8.0, None, ALU.mult, None, [("der", 2)], [("der", 1)])
    dve_ts(der[:, 9:17], der[:, 17:25], -16.0, None, ALU.mult, None, [("der", 2)], [("der", 1, 2)])
    DER = [("der", 0), ("der", 1), ("der", 1, 2)]

    SO = contextlib.ExitStack()
    BTc = sb("BTc", (128, 12, 128), F32, SO); BTp = sb("BTp", (128, 12, 128), F32, SO); BTt = sb("BTt", (8, 12, 8), F32, SO)
    wvb = [sb("wvb%d" % i, (128, 8, 256), BF16, SO) for i in range(2)]
    SA = contextlib.ExitStack()
    lines = sb("lines", (12, 3, LL + LT), F32, SA)
    relb = sb("relb", (33, 12), F32, SA); cm = sb("cm", (33, 3, LL + LT), F32, SA)
    wv_pre = {}
    for g in range(2):
        s_ = rot("wvb", 2)
        dma("pool", wvb[s_][:], D["wv"].ap()[0, g], [], [("wvb", s_)], ("wvb", s_))
        wv_pre[(0, g)] = s_
    dma("sp", relb[:], D["relb"].ap(), [], ["relb"], "c4")
    dma("sp", cm[:], D["cmat"].ap(), [], ["cm"], "c5")
    xT = sb("xT", (128, 8, NT), F32, SA)
    for kc in range(8):
        dma("sp", xT[:, kc, :], D["xT"].ap()[:, kc, :], [], [("xT", kc)], ("xT", kc))
    for g in range(3):
        b = 5 + g
        mm(ps[b][0:12, 0:LL + LT], relb[:], cm[:, g, :], True, True, ["relb", "cm"], [PB(b)])
        act(lines[:, g, :], ps[b][0:12, 0:LL + LT], AF.Identity, [PB(b)], [("lines", g)])
    for g in range(3):
        dma("sp", scr1.ap()[4 * g:4 * g + 4, :], lines[4 * g:4 * g + 4, g, :], [("lines", g)], [("scr1", g)], ("c6", g))
    dma("sp", scr.ap(), bass.AP(scr1, 0, [[LLT, 12], [0, 128], [1, LLT]]), [("scr1", g) for g in range(3)], ["scr"], "c7")
    dma("sp", BTc[:], bass.AP(scr, 128, [[LLT - 1, 128], [128 * LLT, 12], [1, 128]]), ["scr"], ["BTc"], "c8")
    dma("sp", BTp[:], bass.AP(scr, 256, [[LLT - 1, 128], [128 * LLT, 12], [1, 128]]), ["scr"], ["BTp"], "c9")
    dma("sp", BTt[:], bass.AP(scr, LL + 8, [[LLT - 1, 8], [128 * LLT, 12], [1, 8]]), ["scr"], ["BTt"], "c10")
    for k_ in ("c7", "c8", "c9", "c10"):
        P.nosbuf_keys.add(k_)

    with SA:
        sq = [sb("sq%d" % i, (128, NT), BF16, SA) for i in range(2)]
        sd = sb("sd", (128, NT), F32, SA)
        rstd = sb("rstd", (128, NT), F32, SA)
        for kc in range(8):
            s = kc % 2
            act(sq[s][:], xT[:, kc, :], AF.Square, [("xT", kc)], [("sq", s)])
            for ti, (t0, n) in enumerate(TCS):
                mm(ps[ti][:, 0:n], ones_b[:], sq[s][:, t0:t0 + n], kc == 0, kc == 7, ["ones_b", ("sq", s)], [PB(ti)])
        for ti, (t0, n) in enumerate(TCS):
            act(sd[:, t0:t0 + n], ps[ti][:, 0:n], AF.Ln, [PB(ti)], [("sd", ti)], scale=1.0 / 1024, bias=EPS)
            act(rstd[:, t0:t0 + n], sd[:, t0:t0 + n], AF.Exp, [("sd", ti)], [("rstd", ti)], scale=-0.5)
        for kc in range(8):
            dve_stt(uT[:, kc, :], xT[:, kc, :], par[:, kc:kc + 1], rstd[:], ALU.mult, ALU.mult,
                    [("xT", kc), "par"] + [("rstd", ti) for ti in range(5)], [("uT", kc)])
    bank_ctr[0] = 0
    UT = [("uT", kc) for kc in range(8)]
    if KSTOP <= 1:
        P.run(); return nc
    P.barrier()

    with contextlib.ExitStack() as SB:
        Vb = sb("Vb", (128, 3, 16, 256), BF16, SB)
        vst = [sb("vst%d" % i, (128, 256), F32, SB) for i in range(4)]
        Vsb = sb("Vsb", (8, 4, 3, 512), BF16, SB); Vsf = [sb("Vsf%d" % i, (8, 256), F32, SB) for i in range(2)]
        kss = [sb("kss%d" % i, (32, 128), F32, SB) for i in range(2)]
        qT = [sb("qT%d" % g, (128, NPR), BF16, SB) for g in range(3)]
        kT = [sb("kT%d" % g, (128, NPR), BF16, SB) for g in range(3)]
        kTs = sb("kTs", (128, 12, 32), BF16, SB)
        sgb = sb("sgb", (128, NPR), F32, SB)
        sqb = [KVmV[3], KVmV[4], KVmV[5]]
        rsq = [sb("rsq%d" % i, (128, 512), F32, SB) for i in range(3)]
        kf = [sb("kf%d" % i, (128, 512), F32, SB) for i in range(3)]
        kst = [sb("kst%d" % i, (128, 4, 128), F32, SB) for i in range(2)]
        Pf = [sb("Pf%d" % i, (128, 512), F32, SB) for i in range(3)]
        PT = [KVmV[0], KVmV[1], KVmV[2]]
        rden = sb("rden", (128, 512), F32, SB); otmp = sb("otmp", (128, 512), F32, SB)

        def tok_slice(g, i):
            if g == 0:
                return slice(128 * i, 128 * i + 128, 1)
            if g == 1:
                r, n = i // 4, i % 4
                s0 = r + 512 * n
                return slice(s0, s0 + 4 * 128, 4)
            return slice(i, i + 16 * 128, 16)

        def cm_view(t2d, g, c):
            d = DIL[g]
            if d == 1:
                return t2d[:, 512 * c:512 * c + 512]
            v = t2d[:, :].rearrange("p (r m) -> p m r", r=d)
            return v[:, (512 // d) * c:(512 // d) * (c + 1), :]

        def nat_view(ap2d, g):
            d = DIL[g]
            if d == 1:
                return ap2d
            return ap2d.rearrange("p (m r) -> p m r", r=d)

        for hp in range(2):
            def v_load(g, hp=hp):
                if (hp, g) in wv_pre:
                    return wv_pre[(hp, g)]
                s_ = rot("wvb", 2)
                dma("pool", wvb[s_][:], D["wv"].ap()[hp, g], [], [("wvb", s_)], ("wvb", s_))
                return s_

            def v_tile(g, i, s, hp=hp):
                b = bank()
                sl = tok_slice(g, i)
                for kc in range(8):
                    mm(ps[b][:, 0:256], uT[:, kc, sl], wvb[s][:, kc, :], kc == 0, kc == 7, [("uT", kc), ("wvb", s)], [PB(b)])
                act(Vb[:, g, i, :], ps[b][:, 0:256], AF.Copy, [PB(b)], [("Vb", g, i)])
                inwin = (g == 0 and i == 15) or (g == 1 and i % 4 == 3) or g == 2
                if inwin:
                    v = rot("vst", 4)
                    dve_cp(vst[v][:], ps[b][:, 0:256], [PB(b)], [("vst", v)])
                    c0 = 512 + hp * 256
                    if g == 0:
                        dst = kvp[0].ap()[0:128, c0:c0 + 256]
                    elif g == 1:
                        dst = kvp[1].ap()[(i // 4)::4, c0:c0 + 256]
                    else:
                        dst = kvp[2].ap()[i::16, c0:c0 + 256]
                    dma("sp", dst, vst[v][:], [("vst", v)], [], ("vst", v), is_out=True)

            def v_sample(g, s, hp=hp):
                for bb in range(4):
                    b = bank()
                    for kc in range(8):
                        mm(ps[b][0:8, 0:256], uT[:, kc, NPR + 8 * bb:NPR + 8 * bb + 8], wvb[s][:, kc, :], kc == 0, kc == 7,
                           [("uT", kc), ("wvb", s)], [PB(b)])
                    act(Vsb[:, bb, g, hp * 256:hp * 256 + 256], ps[b][0:8, 0:256], AF.Copy, [PB(b)], [("Vsb", bb, g, hp)])
                    vs_ = rot("Vsf", 2)
                    dve_cp(Vsf[vs_][:], ps[b][0:8, 0:256], [PB(b)], [("Vsf", vs_)])
                    dma("sp", kvs[g].ap()[bb, WIN[g] - 8:WIN[g], 512 + hp * 256:512 + hp * 256 + 256], Vsf[vs_][:], [("Vsf", vs_)],
                        [("kvsn2", g, bb, hp)], ("Vsf", vs_), is_out=True)

            s0 = v_load(0)
            s1_ = v_load(1)
            for i in range(16):
                v_tile(0, i, s0)
            v_sample(0, s0)
            s2_ = v_load(2)
            for i in range(16):
                v_tile(1, i, s1_)
                v_tile(2, i, s2_)
            v_sample(1, s1_)
            v_sample(2, s2_)

            if KSTOP <= 3:
                P.run(); return nc
            for hl in range(2):
                h = 2 * hp + hl
                tasks = [(j, ti) for j in range(7) for ti in range(5)]
                slots = {}

                def qk_s1(task, h=h, slots=slots):
                    j, ti = task
                    t0, n = TCS[ti]
                    if ti == 0:
                        slots[j] = ring_load(D["wqk"].ap()[h, j])
                        cc_next()
                    s = slots[j]
                    A = bank()
                    for kc in range(8):
                        mm(ps[A][:, 0:n], ring[s][:, kc, :], uT[:, kc, t0:t0 + n], kc == 0, kc == 7, [("ring", s), ("uT", kc)], [PB(A)])
                    if j == 6:
                        if ti < 4:
                            act(sgb[:, t0:t0 + n], ps[A][:, 0:n], AF.Silu, [PB(A)], [("sgb", ti)])
                        else:
                            act(sgbS[:, h, :], ps[A][:, 0:n], AF.Silu, [PB(A)], [("sgbS", h)])
                        return None
                    x = rot("sqb", 3)
                    act(sqb[x][:, 0:n], ps[A][:, 0:n], AF.Square, [PB(A)], [("sqb", x)])
                    return (A, x)

                def qk_s2(task, st, h=h):
                    if st is None:
                        return
                    j, ti = task
                    t0, n = TCS[ti]
                    A, x = st
                    Bk = bank()
                    mm(ps[Bk][:, 0:n], ones_b[:], sqb[x][:, 0:n], True, True, ["ones_b", ("sqb", x)], [PB(Bk)])
                    act(rsq[x][:, 0:n], ps[Bk][:, 0:n], AF.Ln, [PB(Bk)], [("rsq", x)], scale=1.0 / 128, bias=EPS)
                    act(rsq[x][:, 0:n], rsq[x][:, 0:n], AF.Exp, [("rsq", x)], [("rsq", x)], scale=-0.5)
                    g = j % 3
                    gh = g * 4 + h
                    if j < 3:
                        if ti < 4:
                            dve_stt(cm_view(qT[g], g, ti), nat_view(ps[A][:, 0:512], g), der[:, 0:1], nat_view(rsq[x][:, 0:512], g),
                                    ALU.mult, ALU.mult, [PB(A), ("rsq", x)] + DER, [("qT", g, ti)])
                        else:
                            dve_stt(qTs[:, gh, :], ps[A][:, 0:n], der[:, 0:1], rsq[x][:, 0:n], ALU.mult, ALU.mult,
                                    [PB(A), ("rsq", x)] + DER, [("qTs", gh)])
                        return
                    tiles = [i for i in range(4) if (g == 2 or (g == 1 and ti == 3) or (g == 0 and ti == 3 and i == 3))] if ti < 4 else []
                    if ti < 4 and not tiles:
                        dve_stt(cm_view(kT[g], g, ti), nat_view(ps[A][:, 0:512], g), par[:, 89:90], nat_view(rsq[x][:, 0:512], g),
                                ALU.mult, ALU.mult, [PB(A), ("rsq", x), "par"], [("kT", g, ti)])
                        return None
                    y = rot("kf", 3)
                    dve_stt(kf[y][:, 0:n], ps[A][:, 0:n], par[:, 89:90], rsq[x][:, 0:n], ALU.mult, ALU.mult,
                            [PB(A), ("rsq", x), "par"], [("kf", y)])
                    if ti < 4:
                        dve_cp(cm_view(kT[g], g, ti), nat_view(kf[y][:, 0:512], g), [("kf", y)], [("kT", g, ti)])

                        def s3(tiles=tiles, y=y, g=g, ti=ti):
                            C = bank()
                            for i in tiles:
                                P.add("pe", (lambda e, C=C, i=i, y=y: e.transpose(ps[C][:, 128 * i:128 * i + 128],
                                                                               kf[y][:, 128 * i:128 * i + 128], ident_f[:])),
                                      reads=[("kf", y), "ident_f"], writes=[PB(C)])
                            z = rot("kst", 2)
                            i0, ni = tiles[0], len(tiles)
                            act(kst[z][:, i0:i0 + ni, :], ps[C][:, 128 * i0:128 * (i0 + ni)].rearrange("p (i d) -> p i d", d=128),
                                AF.Copy, [PB(C)], [("kst", z)])
                            tok0 = 512 * ti + 128 * i0 - (NPR - WIN[g])
                            dst = bass.AP(kvp[g], tok0 * 1024 + h * 128, [[1024, 128], [128 * 1024, ni], [1, 128]])
                            dma("sp", dst, kst[z][:, i0:i0 + ni, :], [("kst", z)], [], ("kst", z), is_out=True)
                        return s3
                    else:
                        act(kTs[:, gh, :], kf[y][:, 0:n], AF.Copy, [("kf", y)], [("kTs", gh)])

                        def s3s(y=y, g=g):
                            C = bank()
                            P.add("pe", (lambda e, C=C, y=y: e.transpose(ps[C][0:32, 0:128], kf[y][:, 0:32], ident_f[:])),
                                  reads=[("kf", y), "ident_f"], writes=[PB(C)])
                            z = rot("kss", 2)
                            act(kss[z][:], ps[C][0:32, 0:128], AF.Copy, [PB(C)], [("kss", z)])
                            L = WIN[g]
                            for bb in range(4):
                                dma("sp", kvs[g].ap()[bb, L - 8:L, h * 128:h * 128 + 128], kss[z][8 * bb:8 * bb + 8, :], [("kss", z)],
                                    [("kvsn", g, h, bb)], ("kss", z), is_out=True)
                        return s3s

                prev1 = None
                prev2 = None
                for task in tasks:
                    st = qk_s1(task)
                    if prev2 is not None:
                        prev2()
                        prev2 = None
                    if prev1 is not None:
                        prev2 = qk_s2(*prev1)
                    prev1 = (task, st)
                if prev2 is not None:
                    prev2()
                prev2 = qk_s2(*prev1)
                if prev2 is not None:
                    prev2()

                if KSTOP <= 4:
                    P.run(); return nc
                for c in range(4):
                    Ob = bank(True); Db = bank(True)
                    fl = {"o": True, "d": True}
                    QK_R = lambda g: [("kT", g, t) for t in range(4)] + [("qT", g, t) for t in range(4)]

                    def mk_unit(g, prev, c=c, h=h, hl=hl, Ob=Ob, Db=Db, fl=fl):
                        gh = g * 4 + h
                        u = {}
                        if g < 2:
                            blocks = []
                            for i in range(4):
                                if g == 0:
                                    n_ = 4 * c + i
                                    if prev and n_ == 0:
                                        continue
                                    qs = slice(128 * n_, 128 * n_ + 128)
                                    ks = slice(128 * (n_ - prev), 128 * (n_ - prev) + 128)
                                    vt = n_ - prev
                                else:
                                    qs = slice(i * 512 + 128 * c, i * 512 + 128 * c + 128)
                                    ks = slice(i * 512 + 128 * (c - prev), i * 512 + 128 * (c - prev) + 128)
                                    vt = i * 4 + c - prev
                                blocks.append((i, vt, qs, ks))
                            i0 = blocks[0][0]
                            nb = len(blocks)

                            def S(u=u):
                                Sb = bank()
                                u["Sb"] = Sb
                                for (i, vt, qs, ks) in blocks:
                                    mm(ps[Sb][:, 128 * i:128 * i + 128], kT[g][:, ks], qT[g][:, qs], True, True, QK_R(g), [PB(Sb)])

                            def soft(u=u):
                                Sb = u["Sb"]
                                x = rot("Pf", 3)
                                u["x"] = x
                                BT = BTp if prev else BTc
                                dve_tt(Pf[x][:, 128 * i0:512].rearrange("p (i q) -> p i q", q=128),
                                       ps[Sb][:, 128 * i0:512].rearrange("p (i q) -> p i q", q=128),
                                       bcast_mid(BT[:, gh, :], nb), ALU.add, [PB(Sb), "BTc", "BTp"], [("Pf", x)])
                                act(PT[x][:, 128 * i0:512], Pf[x][:, 128 * i0:512], AF.Exp, [("Pf", x)], [("PT", x)])

                            def pvf(u=u):
                                x = u["x"]
                                for (i, vt, qs, ks) in blocks:
                                    oc = ps[Ob][:, 128 * i:128 * i + 128] if g == 0 else ps[Ob][:, i::4]
                                    mm(oc, Vb[:, g, vt, hl * 128:hl * 128 + 128], PT[x][:, 128 * i:128 * i + 128], fl["o"], False,
                                       [("PT", x), ("Vb", g, vt)], [PB(Ob)])
                                    fl["o"] = False
                                if g == 0:
                                    mm(ps[Db][:, 128 * i0:512], ones_b[:], PT[x][:, 128 * i0:512], fl["d"], False, [("PT", x), "ones_b"], [PB(Db)])
                                    fl["d"] = False
                                else:
                                    for (i, vt, qs, ks) in blocks:
                                        mm(ps[Db][:, i::4], ones_b[:], PT[x][:, 128 * i:128 * i + 128], fl["d"], False, [("PT", x), "ones_b"], [PB(Db)])
                                        fl["d"] = False
                        else:
                            KK = 32 * (c + 1)

                            def S(u=u):
                                Sb = bank()
                                u["Sb"] = Sb
                                for r in range(16):
                                    mm(ps[Sb][0:KK, 32 * r:32 * r + 32], kT[2][:, r * 128:r * 128 + KK],
                                       qT[2][:, r * 128 + 32 * c:r * 128 + 32 * c + 32], True, True, QK_R(2), [PB(Sb)])

                            def soft(u=u):
                                Sb = u["Sb"]
                                x = rot("Pf", 3)
                                u["x"] = x
                                dve_tt(Pf[x][0:KK, :].rearrange("p (r q) -> p r q", q=32), ps[Sb][0:KK, :].rearrange("p (r q) -> p r q", q=32),
                                       bcast_mid(BTc[0:KK, gh, 32 * c:32 * c + 32], 16), ALU.add, [PB(Sb), "BTc"], [("Pf", x)])
                                act(PT[x][0:KK, :], Pf[x][0:KK, :], AF.Exp, [("Pf", x)], [("PT", x)])

                            def pvf(u=u):
                                x = u["x"]
                                for r in range(16):
                                    mm(ps[Ob][:, r::16], Vb[0:KK, 2, r, hl * 128:hl * 128 + 128], PT[x][0:KK, 32 * r:32 * r + 32], fl["o"], False,
                                       [("PT", x), ("Vb", 2, r)], [PB(Ob)])
                                    fl["o"] = False
                                for r in range(16):
                                    mm(ps[Db][:, r::16], ones_b[0:KK, :], PT[x][0:KK, 32 * r:32 * r + 32], fl["d"], False, [("PT", x), "ones_b"], [PB(Db)])
                                    fl["d"] = False
                        u["S"], u["soft"], u["pv"] = S, soft, pvf
                        return u

                    units = [mk_unit(0, 0), mk_unit(0, 1), mk_unit(1, 0)] + ([mk_unit(1, 1)] if c > 0 else []) + [mk_unit(2, 0)]
                    units[0]["S"](); units[1]["S"]()
                    for k, u in enumerate(units):
                        u["soft"]()
                        if k + 2 < len(units):
                            units[k + 2]["S"]()
                        u["pv"]()
                    act(rden[:], ps[Db][:, :], AF.Ln, [PB(Db)], ["rden"])
                    act(rden[:], rden[:], AF.Exp, ["rden"], ["rden"], scale=-1.0)
                    dve_tt(otmp[:], rden[:], sgb[:, 512 * c:512 * c + 512], ALU.mult, ["rden", ("sgb", c)], ["otmp"])
                    dve_tt(obg[:, h, 512 * c:512 * c + 512], ps[Ob][:, :], otmp[:], ALU.mult, [PB(Ob), "otmp"], [("obg", h, c)])
                    reserved.discard(Ob); reserved.discard(Db)

        if KSTOP <= 5:
            P.run(); return nc
        dve_cp(BTps[:], BTp[:, :, 0:8], ["BTp"], ["BTps"])
        Os = bank(True)
        sfirst = {"o": True}

        def spv(cols, vap, ptap, K, reads):
            first = sfirst["o"]
            sfirst["o"] = False
            mm(cols(0), vap, ptap, first, False, reads, [PB(Os)])
            mm(cols(128), ones_b[0:K, :], ptap, False, False, reads + ["ones_b"], [PB(Os)])

        QS = [("qTs", i) for i in range(12)]
        KS = [("kTs", i) for i in range(12)]
        for bb in range(4):
            Sb = bank()
            for gh in range(12):
                mm(ps[Sb][0:8, 8 * gh:8 * gh + 8], kTs[:, gh, 8 * bb:8 * bb + 8], qTs[:, gh, 8 * bb:8 * bb + 8], True, True, QS + KS, [PB(Sb)])
            x = rot("Pf", 3)
            dve_tt(Pf[x][0:8, 0:96], ps[Sb][0:8, 0:96], BTt[:, :, :].rearrange("p a b -> p (a b)"), ALU.add, [PB(Sb), "BTt"], [("Pf", x)])
            act(PT[x][0:8, 0:96], Pf[x][0:8, 0:96], AF.Exp, [("Pf", x)], [("PT", x)])
            for gh in range(12):
                g, h = gh // 4, gh % 4
                spv((lambda off, h=h, bb=bb: ps[Os][:, off + h * 32 + 8 * bb:off + h * 32 + 8 * bb + 8]),
                    Vsb[:, bb, g, h * 128:h * 128 + 128], PT[x][0:8, 8 * gh:8 * gh + 8], 8,
                    [("PT", x), ("Vsb", bb, g, 0), ("Vsb", bb, g, 1)])
        stiles = [(bb, g, r) for bb in range(4) for g in range(3) for r in range((1, 4, 8)[g])]

        def sm_L(t):
            bb, g, r = stiles[t]
            kk = rot("KVmK", 5)
            kv = rot("KVmV", 9)
            dma("pool", KVmK[kk][:], caches[g].ap()[bb, r::DIL[g], 0:512], [], [("KVmK", kk)], ("KVmK", kk))
            dma("pool", KVmV[kv][:], caches[g].ap()[bb, r::DIL[g], 512:1024], [], [("KVmV", kv)], ("KVmV", kv))
            us[t] = {"kk": kk, "kv": kv}

        def sm_T(t):
            u = us[t]
            kk = u["kk"]
            Tb = bank()
            psb = ps[Tb][:, 0:256].bitcast(BF16)
            for h in range(4):
                P.add("pe", (lambda e, psb=psb, kk=kk, h=h: e.transpose(psb[:, 128 * h:128 * h + 128],
                                                                       KVmK[kk][:, 128 * h:128 * h + 128], ident_b[:])),
                      reads=[("KVmK", kk), "ident_b"], writes=[PB(Tb)])
            km = rot("kTm", 3)
            dve_cp(kTm[km][:], psb, [PB(Tb)], [("kTm", km)])
            u["km"] = km

        def sm_S(t):
            bb, g, r = stiles[t]
            u = us[t]
            nq = (8, 2, 1)[g]
            km = u["km"]
            Sb = bank()
            qsl = slice(8 * bb, 8 * bb + 8) if g == 0 else (slice(8 * bb + r, 8 * bb + r + 5, 4) if g == 1 else slice(8 * bb + r, 8 * bb + r + 1))
            for h in range(4):
                mm(ps[Sb][:, 8 * h:8 * h + nq], kTm[km][:, 128 * h:128 * h + 128], qTs[:, g * 4 + h, qsl], True, True,
                   [("kTm", km)] + QS, [PB(Sb)])
            x = rot("Pfs", 3)
            if g == 2:
                for h in range(4):
                    act(PTs[x][:, 8 * h:8 * h + 1], ps[Sb][:, 8 * h:8 * h + 1], AF.Exp, [PB(Sb), "BTps"],
                        [("PTs", x, h)] + ([("PTs", x)] if h == 3 else []), bias=BTps[:, 8 + h, 0:1])
            else:
                dve_tt(Pfs[x][:, :].rearrange("p (h a) -> p h a", a=8)[:, :, 0:nq],
                       ps[Sb][:, 0:32].rearrange("p (h a) -> p h a", a=8)[:, :, 0:nq],
                       BTps[:, 4 * g:4 * g + 4, 0:nq], ALU.add, [PB(Sb), "BTps"], [("Pfs", x)])
                act(PTs[x][:, :].rearrange("p (h a) -> p h a", a=8)[:, :, 0:nq],
                    Pfs[x][:, :].rearrange("p (h a) -> p h a", a=8)[:, :, 0:nq], AF.Exp, [("Pfs", x)],
                    [("PTs", x)] + [("PTs", x, h) for h in range(4)])
            u["x"], u["qsl"], u["nq"] = x, qsl, nq

        def sm_PV(t):
            u = us[t]
            kv, x, qsl, nq = u["kv"], u["x"], u["qsl"], u["nq"]
            for h in range(4):
                def cols(off, h=h, qsl=qsl):
                    return ps[Os][:, slice(off + h * 32 + qsl.start, off + h * 32 + qsl.stop, qsl.step)]
                spv(cols, KVmV[kv][:, 128 * h:128 * h + 128], PTs[x][:, 8 * h:8 * h + nq], 128,
                    [("PTs", x), ("PTs", x, h), ("KVmV", kv)])

        us = {}
        nt_ = len(stiles)
        sm_k = [0]

        def sm_step():
            k = sm_k[0]
            sm_k[0] += 1
            if k >= nt_ + 7:
                return False
            if 0 <= k - 7 < nt_:
                sm_PV(k - 7)
            if 0 <= k - 5 < nt_:
                sm_S(k - 5)
            if 0 <= k - 3 < nt_:
                sm_T(k - 3)
            if k < nt_:
                sm_L(k)
            return True

        def sm_finish():
            while sm_step():
                pass
            act(rdens[:], ps[Os][:, 128:256], AF.Ln, [PB(Os)], ["rdens"])
            act(rdens[:], rdens[:], AF.Exp, ["rdens"], ["rdens"], scale=-1.0)
            dve_tt(otmps[:], rdens[:], sgbS[:, :, :].rearrange("p a b -> p (a b)"), ALU.mult,
                   ["rdens"] + [("sgbS", hh) for hh in range(4)], ["otmps"])
            dve_tt(obg[:, :, NPR:NT], ps[Os][:, 0:128].rearrange("p (a b) -> p a b", b=32), otmps[:].rearrange("p (a b) -> p a b", b=32),
                   ALU.mult, [PB(Os), "otmps"], [("obg", "s")])
            reserved.discard(Os)
    SO.close()
    OBG = [("obg", h, c) for h in range(4) for c in range(4)] + [("obg", "s")]
    if KSTOP <= 6:
        P.run(); return nc
    P.barrier()

    hsg = sb("hsg", (128, 8, NT), BF16)
    while cc_pieces:
        cc_next()
    with contextlib.ExitStack() as SC:
        wab = sb("wab", (128, 8, 128), BF16, SC); wxb = sb("wxb", (128, 8, 128), BF16, SC)
        dma("pool", wab[:], D["wa"].ap(), [], ["wab"], "wab")
        dma("pool", wxb[:], D["wx"].ap(), [], ["wxb"], "wxb")
        xaf = [sb("xaf%d" % i, (128, 3 + NPR), F32, SC) for i in range(2)]
        xs = [sb("xs%d" % i, (128, 4, 11), F32, SC) for i in range(2)]
        xc = [sb("xc%d" % i, (128, NT), F32, SC) for i in range(2)]
        xcb = [sb("xcb%d" % i, (128, NT), BF16, SC) for i in range(2)]
        t1 = sb("t1", (128, NT), F32, SC); af = sb("af", (128, NT), F32, SC); iff = sb("iff", (128, NT), F32, SC)
        hs = sb("hs", (128, NT), F32, SC); sga = sb("sga", (128, NT), F32, SC)
        for q in range(2):
            P.add("dve", (lambda e, q=q: e.memset(xaf[q][:, 0:3], 0.0)), writes=[("xaf0", q)])
        HB = [(0, 1024), (1024, NT)]
        HP = [(0, 1024), (1024, NPR)]
        HT = [[0, 1], [2, 3, 4]]
        T1h = [[("t1", ti) for ti in HT[h_]] for h_ in range(2)]
        IFh = [[("iff", ti) for ti in HT[h_]] for h_ in range(2)]
        SGh = [[("sga", ti) for ti in HT[h_]] for h_ in range(2)]
        HSS = [("hss", bb) for bb in range(4)]
        HS = [("hsp", 0), ("hsp", 1)] + HSS
        slot2 = {}

        def XAh(q, h_):
            return ([("xaf", q, 0), ("xaf", q, 1), ("xaf0", q)] if h_ == 0 else [("xaf", q, 1), ("xaf", q, 2), ("xaf", q, 3)])

        def lru_s1a(kc):
            q = kc % 2
            s = ring_load(D["wlru"].ap()[kc, 0])
            slot2[kc] = ring_load(D["wlru"].ap()[kc, 1])
            act(xs[q][:, :, 0:3], scv[:, kc, :, :], AF.Copy, ["scv"], [("xs", q)])
            for ti, (t0, n) in enumerate(TCS):
                A = bank()
                for k2 in range(8):
                    mm(ps[A][:, 0:n], ring[s][:, k2, :], uT[:, k2, t0:t0 + n], k2 == 0, k2 == 7, [("ring", s), ("uT", k2)], [PB(A)])
                if ti < 4:
                    act(xaf[q][:, 3 + t0:3 + t0 + n], ps[A][:, 0:n], AF.Copy, [PB(A)], [("xaf", q, ti)])
                else:
                    act(xs[q][:, :, 3:11], ps[A][:, 0:32].rearrange("p (b t) -> p b t", t=8), AF.Copy, [PB(A)], [("xs2", q)])
                if ti in (1, 3):
                    h_ = ti // 2
                    c0, c1 = HP[h_]
                    act(xc[q][:, c0:c1], xaf[q][:, 3 + c0:3 + c1], AF.Identity, XAh(q, h_) + ["par"], [("xcp", q, h_)],
                        scale=par[:, 8 + 3 * 8 + kc:8 + 3 * 8 + kc + 1], bias=par[:, 40 + kc:41 + kc])

        def lru_s1b(kc):
            q = kc % 2
            XA = [("xaf", q, ti) for ti in range(4)] + [("xaf0", q)]
            XS = [("xs", q), ("xs2", q)]
            w = lambda j: par[:, 8 + j * 8 + kc:8 + j * 8 + kc + 1]
            cb = par[:, 40 + kc:41 + kc]
            for h_ in range(2):
                c0, c1 = HP[h_]
                for j in range(3):
                    dve_stt(xc[q][:, c0:c1], xaf[q][:, j + c0:j + c1], w(j), xc[q][:, c0:c1], ALU.mult, ALU.add,
                            XAh(q, h_) + ["par", ("xcp", q, h_)], [("xcp", q, h_)])
                if h_ == 0:
                    dve_cp(xcb[q][:, 0:1024], xc[q][:, 0:1024], [("xcp", q, 0)], [("xcb", q, 0)])
            xcs = xc[q][:, NPR:NT].rearrange("p (b t) -> p b t", t=8)
            dve_ts(xcs, xs[q][:, :, 3:11], w(3), cb, ALU.mult, ALU.add, XS + ["par"], [("xcs", q)])
            for j in range(3):
                dve_stt(xcs, xs[q][:, :, j:j + 8], w(j), xcs, ALU.mult, ALU.add, XS + ["par", ("xcs", q)], [("xcs", q)])
            dve_cp(xcb[q][:, 1024:NT], xc[q][:, 1024:NT], [("xcp", q, 1), ("xcs", q)], [("xcb", q, 1)])
            dve_cp(small[:, kc, 0:3], xaf[q][:, NPR:NPR + 3], XA, [("small", kc, 0)])
            dve_cp(small[:, kc, 4:16].rearrange("p (b t) -> p b t", t=3), xs[q][:, :, 8:11], XS, [("small", kc, 1)])

        def lru_s2a(kc):
            q = kc % 2
            for ti, (t0, n) in enumerate(TCS):
                xb_ = ("xcb", q, 0 if ti < 2 else 1)
                A = bank()
                mm(ps[A][:, 0:n], wab[:, kc, :], xcb[q][:, t0:t0 + n], True, True, ["wab", xb_], [PB(A)])
                act(t1[:, t0:t0 + n], ps[A][:, 0:n], AF.Sigmoid, [PB(A), "par"], [("t1", ti)], bias=par[:, 48 + kc:49 + kc])
                Bk = bank()
                mm(ps[Bk][:, 0:n], wxb[:, kc, :], xcb[q][:, t0:t0 + n], True, True, ["wxb", xb_], [PB(Bk)])
                act(iff[:, t0:t0 + n], ps[Bk][:, 0:n], AF.Sigmoid, [PB(Bk), "par"], [("iff", ti)], bias=par[:, 56 + kc:57 + kc])
            for h_ in range(2):
                c0, c1 = HB[h_]
                act(af[:, c0:c1], t1[:, c0:c1], AF.Exp, T1h[h_] + DER, [("af", h_)], scale=der[:, 1 + kc:2 + kc])
            for h_ in range(2):
                c0, c1 = HB[h_]
                act(t1[:, c0:c1], t1[:, c0:c1], AF.Exp, T1h[h_] + DER, T1h[h_], scale=der[:, 9 + kc:10 + kc])
            for h_ in range(2):
                c0, c1 = HB[h_]
                act(t1[:, c0:c1], t1[:, c0:c1], AF.Sqrt, T1h[h_], T1h[h_], scale=-1.0, bias=1.0)

        def lru_s2b(kc):
            q = kc % 2
            for h_ in range(2):
                c0, c1 = HB[h_]
                p0, p1 = HP[h_]
                xr = [("xcp", q, h_)] + ([("xcs", q)] if h_ == 1 else [])
                dve_tt(iff[:, c0:c1], iff[:, c0:c1], xc[q][:, c0:c1], ALU.mult, IFh[h_] + xr, IFh[h_])
                dve_tt(iff[:, c0:c1], iff[:, c0:c1], t1[:, c0:c1], ALU.mult, IFh[h_] + T1h[h_], IFh[h_])
                if h_ == 0:
                    P.add("dve", lambda e: e.tensor_tensor_scan(out=hs[:, 0:1024], data0=af[:, 0:1024], data1=iff[:, 0:1024], initial=0.0,
                                                                op0=ALU.mult, op1=ALU.add), reads=[("af", 0)] + IFh[0], writes=[("hsp", 0)])
                else:
                    P.add("dve", lambda e: e.tensor_tensor_scan(out=hs[:, 1024:NPR], data0=af[:, 1024:NPR], data1=iff[:, 1024:NPR],
                                                                initial=hs[:, 1023:1024], op0=ALU.mult, op1=ALU.add),
                          reads=[("af", 1), ("hsp", 0)] + IFh[1], writes=[("hsp", 1)])
            for bb in range(4):
                c0 = NPR + 8 * bb
                P.add("dve", (lambda e, c0=c0, bb=bb, kc=kc: e.tensor_tensor_scan(out=hs[:, c0:c0 + 8], data0=af[:, c0:c0 + 8],
                                                                                data1=iff[:, c0:c0 + 8], initial=shh[:, kc, bb:bb + 1],
                                                                                op0=ALU.mult, op1=ALU.add)),
                      reads=[("af", 1), "shh"] + IFh[1], writes=[("hss", bb)])
            dve_cp(small[:, kc, 3:4], hs[:, NPR - 1:NPR], HS, [("small", kc, 2)])
            dve_cp(small[:, kc, 16:20], hs[:, NPR + 7:NT:8], HS, [("small", kc, 3)])

        def lru_s2c(kc):
            s2 = slot2[kc]
            for ti, (t0, n) in enumerate(TCS):
                A = bank()
                for k2 in range(8):
                    mm(ps[A][:, 0:n], ring[s2][:, k2, :], uT[:, k2, t0:t0 + n], k2 == 0, k2 == 7, [("ring", s2), ("uT", k2)], [PB(A)])
                act(sga[:, t0:t0 + n], ps[A][:, 0:n], AF.Silu, [PB(A)], [("sga", ti)])

        def lru_s2d(kc):
            dve_tt(hsg[:, kc, 0:1024], hs[:, 0:1024], sga[:, 0:1024], ALU.mult, [("hsp", 0)] + SGh[0], [("hsg", kc)])
            dve_tt(hsg[:, kc, 1024:NT], hs[:, 1024:NT], sga[:, 1024:NT], ALU.mult, [("hsp", 1)] + HSS + SGh[1], [("hsg", kc)])

        sm_step(); sm_step()
        lru_s1a(0)
        sm_step()
        lru_s1b(0)
        for kc in range(8):
            sm_step()
            if kc + 1 < 8:
                lru_s1a(kc + 1)
            sm_step()
            lru_s2a(kc)
            sm_step()
            if kc + 1 < 8:
                lru_s1b(kc + 1)
            sm_step()
            lru_s2c(kc)
            sm_step()
            lru_s2b(kc)
            sm_step()
            lru_s2d(kc)
            sm_step()
        sm_finish()
        dma("sp", D["small"].ap(), small[:], [("small", kc, i) for kc in range(8) for i in range(4)], [], "small", is_out=True)
    if KSTOP <= 7:
        P.run(); return nc
    P.barrier()

    merged = sb("merged", (128, 8, NT), BF16)
    wo = sb("wo", (128, 8, 1024), BF16)
    dma("pool", wo[:], D["wout"].ap(), [], ["wo"], "wo")
    xk = [sb("xk%d" % i, (128, 1024), F32) for i in range(2)]
    xk_pre = {}
    for tt in range(2):
        s_ = rot("xk", 2)
        dma("sp", xk[s_][0:128, :], D["xtok"].ap()[128 * tt:128 * tt + 128, :], [], [("xk", s_)], ("xk", s_))
        xk_pre[tt] = s_
    with contextlib.ExitStack() as SD:
        wf = [sb("wf%d" % i, (128, 28, 128), BF16, SD) for i in range(2)]
        s1 = [sb("s1_%d" % i, (128, 512), F32, SD) for i in range(2)]
        s2t = [sb("s2_%d" % i, (128, 512), F32, SD) for i in range(2)]
        m1 = [sb("m1_%d" % i, (128, 512), F32, SD) for i in range(2)]
        m2 = [sb("m2_%d" % i, (128, 512), F32, SD) for i in range(2)]
        for dc in range(8):
            s = rot("wf", 2)
            dma("pool", wf[s][:], D["wfin"].ap()[dc], [], [("wf", s)], ("wf", s))
            for ti, (t0, n) in enumerate(TCS):
                Ya = bank(); Yb = bank(); G1 = bank(); G2 = bank()
                for kc in range(8):
                    mm(ps[Ya][:, 0:n], wf[s][:, kc, :], hsg[:, kc, t0:t0 + n], kc == 0, kc == 7, [("wf", s), ("hsg", kc)], [PB(Ya)])
                for k4 in range(4):
                    mm(ps[Yb][:, 0:n], wf[s][:, 8 + k4, :], obg[:, k4, t0:t0 + n], k4 == 0, k4 == 3, [("wf", s)] + OBG, [PB(Yb)])
                for kc in range(8):
                    mm(ps[G1][:, 0:n], wf[s][:, 12 + kc, :], uT[:, kc, t0:t0 + n], kc == 0, kc == 7, [("wf", s), ("uT", kc)], [PB(G1)])
                for kc in range(8):
                    mm(ps[G2][:, 0:n], wf[s][:, 20 + kc, :], uT[:, kc, t0:t0 + n], kc == 0, kc == 7, [("wf", s), ("uT", kc)], [PB(G2)])
                x = rot("mg", 2)
                act(s1[x][:, 0:n], ps[G1][:, 0:n], AF.Sigmoid, [PB(G1), "par"], [("s1", x)], bias=par[:, 72 + dc:73 + dc])
                act(s2t[x][:, 0:n], ps[G2][:, 0:n], AF.Sigmoid, [PB(G2), "par"], [("s2", x)], bias=par[:, 80 + dc:81 + dc])
                dve_tt(m1[x][:, 0:n], ps[Ya][:, 0:n], s1[x][:, 0:n], ALU.mult, [PB(Ya), ("s1", x)], [("m1", x)])
                dve_tt(m2[x][:, 0:n], ps[Yb][:, 0:n], s2t[x][:, 0:n], ALU.mult, [PB(Yb), ("s2", x)], [("m2", x)])
                dve_tt(merged[:, dc, t0:t0 + n], m1[x][:, 0:n], m2[x][:, 0:n], ALU.add, [("m1", x), ("m2", x)], [("merged", dc)])
    MG = [("merged", dc) for dc in range(8)]
    if KSTOP <= 8:
        P.run(); return nc
    P.barrier()

    with contextlib.ExitStack() as SE:
        xk = xk + [sb("xk%d" % i, (128, 1024), F32, SE) for i in (2, 3)]
        yst = [sb("yst%d" % i, (128, 1024), F32, SE) for i in range(4)]
        for tt in range(17):
            r0 = 128 * tt
            nt = 128 if tt < 16 else 32
            if tt in xk_pre:
                s = xk_pre[tt]
            else:
                s = rot("xk", 4)
                dma("sp", xk[s][0:nt, :], D["xtok"].ap()[r0:r0 + nt, :], [], [("xk", s)], ("xk", s))
            y = rot("yst", 4)
            for half in range(2):
                A = bank()
                for kc in range(8):
                    mm(ps[A][0:nt, :], merged[:, kc, r0:r0 + nt], wo[:, kc, 512 * half:512 * half + 512], kc == 0, kc == 7, MG + ["wo"], [PB(A)])
                dve_tt(yst[y][0:nt, 512 * half:512 * half + 512], ps[A][0:nt, :], xk[s][0:nt, 512 * half:512 * half + 512], ALU.add,
                       [PB(A), ("xk", s)], [("yst", y, half)])
            dma("sp", D["ytok"].ap()[r0:r0 + nt, :], yst[y][0:nt, :], [("yst", y, 0), ("yst", y, 1)], [], ("yst", y), is_out=True)

    P.run()
    ST.close()
    return nc


def _fm(W):
    k = W.shape[0] // 128
    return np.ascontiguousarray(W.reshape(k, 128, W.shape[1]).transpose(1, 0, 2))


_CACHE = {}
_PREP_ONLY = [False]


def kernel(x_prompt, x_sample, cache_kv_w128, cache_kv_w512, cache_kv_w2048, state_conv, state_h,
           g_norm, w_in, b_merge, conv_w, conv_b, lru_w_a, lru_b_a, lru_w_x, lru_b_x, lru_lambda,
           g_q, g_k, rel_bias, w_lru_proj, w_attn_proj, w_out):
    f = lambda a: np.asarray(a, np.float32)
    x_prompt, x_sample = f(x_prompt), f(x_sample)
    W = f(w_in)[0]
    wv = np.stack([np.stack([_fm(W[:, O4 + (g * 4 + 2 * hp) * 128:O4 + (g * 4 + 2 * hp) * 128 + 256]) for g in range(3)]) for hp in range(2)])
    def qkcols(h, j):
        if j < 3:
            return O2 + (j * 4 + h) * 128
        if j < 6:
            return O3 + ((j - 3) * 4 + h) * 128
        return O5 + h * 128
    wqk = np.stack([np.stack([_fm(W[:, qkcols(h, j):qkcols(h, j) + 128]) for j in range(7)]) for h in range(4)])
    wlru = np.stack([np.stack([_fm(W[:, kc * 128:kc * 128 + 128]), _fm(W[:, O1 + kc * 128:O1 + kc * 128 + 128])]) for kc in range(8)])
    wa = np.ascontiguousarray(f(lru_w_a)[0].transpose(1, 0, 2))
    wx = np.ascontiguousarray(f(lru_w_x)[0].transpose(1, 0, 2))
    wlp, wap, wo_ = f(w_lru_proj)[0], f(w_attn_proj)[0], f(w_out)[0]
    wfin = np.stack([np.concatenate([_fm(wlp[:, dc * 128:dc * 128 + 128]), _fm(wap[:, dc * 128:dc * 128 + 128]),
                                     _fm(W[:, O6 + dc * 128:O6 + dc * 128 + 128]),
                                     _fm(W[:, O6 + 1024 + dc * 128:O6 + 1024 + dc * 128 + 128])], axis=1) for dc in range(8)])
    wout = _fm(wo_)
    par = np.zeros((128, NPAR), np.float32)
    v8 = lambda v: np.asarray(v, np.float32).reshape(-1, 128).T
    par[:, 0:8] = v8(f(g_norm)[0])
    cw = f(conv_w)[0]
    for j in range(4):
        par[:, 8 + 8 * j:16 + 8 * j] = v8(cw[j])
    par[:, 40:48] = v8(f(conv_b)[0]); par[:, 48:56] = v8(f(lru_b_a)[0]); par[:, 56:64] = v8(f(lru_b_x)[0])
    par[:, 64:72] = v8(f(lru_lambda)[0]); par[:, 72:88] = v8(f(b_merge)[0])
    par[:, 88] = f(g_q)[0]; par[:, 89] = f(g_k)[0]
    relb = np.concatenate([f(rel_bias), np.ones((1, 12), np.float32)], axis=0)
    cmat = _cmat()
    ident = np.eye(128, dtype=np.float32)
    c128, c512, c2048 = f(cache_kv_w128)[0], f(cache_kv_w512)[0], f(cache_kv_w2048)[0]
    sc_, sh_ = f(state_conv)[0], f(state_h)[0]

    in_maps = []
    for c in range(8):
        xs_ = x_sample[4 * c:4 * c + 4].reshape(32, 1024)
        xtok = np.ascontiguousarray(np.concatenate([x_prompt[c], xs_], axis=0))
        xT = np.ascontiguousarray(xtok.T.reshape(8, 128, NT).transpose(1, 0, 2))
        scc = np.ascontiguousarray(sc_[4 * c:4 * c + 4].reshape(4, 3, 8, 128).transpose(3, 2, 0, 1))
        shc = np.ascontiguousarray(sh_[4 * c:4 * c + 4].reshape(4, 8, 128).transpose(2, 1, 0))
        in_maps.append({
            "xT": xT, "xtok": xtok, "wv": wv, "wqk": wqk, "wlru": wlru, "wa": wa, "wx": wx, "wfin": wfin, "wout": wout,
            "par": par, "relb": relb, "cmat": cmat, "ident": ident,
            "c128": np.ascontiguousarray(c128[4 * c:4 * c + 4].reshape(4, 128, 1024)),
            "c512": np.ascontiguousarray(c512[4 * c:4 * c + 4].reshape(4, 512, 1024)),
            "c2048": np.ascontiguousarray(c2048[4 * c:4 * c + 4].reshape(4, 2048, 1024)),
            "sconv": scc, "sh": shc,
        })
    if _PREP_ONLY[0]:
        return in_maps
    if "nc" not in _CACHE:
        _CACHE["nc"] = build_program()
    nc = _CACHE["nc"]
    ncores = int(os.environ.get("KCORES", "8"))
    res = run_bass_kernel_spmd(nc, in_maps[:ncores], core_ids=list(range(ncores)))
    R = list(res.results)
    while len(R) < 8:
        R.append({k: np.zeros_like(v) for k, v in R[0].items()})
    y_prompt = np.stack([R[c]["ytok"][0:2048] for c in range(8)])
    y_sample = np.concatenate([R[c]["ytok"][2048:].reshape(4, 8, 1024) for c in range(8)], axis=0)
    kvp_o = [np.stack([R[c][nm].reshape(-1, 2, 4, 128) for c in range(8)])[None] for nm in ("kvp128", "kvp512", "kvp2048")]
    kvs_o = [np.concatenate([R[c][nm].reshape(4, -1, 2, 4, 128) for c in range(8)], axis=0)[None] for nm in ("kvs128", "kvs512", "kvs2048")]
    sm = np.stack([R[c]["small"] for c in range(8)])
    smf = sm.transpose(0, 3, 2, 1).reshape(8, 20, 1024)
    conv_prompt = smf[:, 0:3][None]
    h_prompt = smf[:, 3][None]
    conv_sample = smf[:, 4:16].reshape(8, 4, 3, 1024).reshape(32, 3, 1024)[None]
    h_sample = smf[:, 16:20].reshape(32, 1024)[None]
    asf = lambda a: np.ascontiguousarray(a, dtype=np.float32)
    return (asf(y_prompt), asf(y_sample), asf(kvp_o[0]), asf(kvp_o[1]), asf(kvp_o[2]), asf(conv_prompt), asf(h_prompt),
            asf(kvs_o[0]), asf(kvs_o[1]), asf(kvs_o[2]), asf(conv_sample), asf(h_sample))
```

```python
import numpy as np
import concourse.bass as bass
import concourse.mybir as mybir
from concourse.bass_utils import run_bass_kernel_spmd

F32 = mybir.dt.float32
BF16 = mybir.dt.bfloat16
AF = mybir.ActivationFunctionType
ALU = mybir.AluOpType
AX = mybir.AxisListType


class Ins:
    __slots__ = ("eng", "emit", "deps", "tick", "is_dma", "key", "sig", "waits", "seq")


class Prog:
    COMPUTE = ("pe", "act", "dve", "pool")

    def __init__(self, nc):
        self.nc = nc
        self.order = []
        self.lastw = {}
        self.readers = {}
        self.dma_cum = {}
        self.out_keys = set()
        self.last_c = {}
        self.last_d = {}
        self.pending = {}
        self.nosbuf_keys = set()

    def barrier(self):
        snap = list(self.last_c.values()) + [v for k, v in self.last_d.items() if k not in self.nosbuf_keys]
        self.pending = {e: snap for e in ("pe", "act", "dve", "pool", "sp")}

    def add(self, eng, emit, reads=(), writes=(), dma_key=None, out=False):
        ins = Ins()
        ins.eng = eng
        ins.emit = emit
        ins.is_dma = dma_key is not None
        ins.key = dma_key
        ins.sig = False
        ins.tick = 0
        d = {}
        for r in reads:
            w = self.lastw.get(r)
            if w is not None:
                d[w] = "RAW"
            if isinstance(r, tuple) and r[0] == "ps":
                for rd in self.readers.get(r, ()):
                    if rd.eng != eng and rd not in d:
                        d[rd] = "WAR"
        for r in writes:
            w = self.lastw.get(r)
            if w is not None and w not in d:
                d[w] = "WAW"
            for rd in self.readers.get(r, ()):
                if rd not in d:
                    d[rd] = "WAR"
        for r in reads:
            self.readers.setdefault(r, []).append(ins)
        for r in writes:
            self.lastw[r] = ins
            self.readers[r] = []
        deps = []
        for dep, kind in d.items():
            if dep is ins:
                continue
            if (not dep.is_dma) and (not ins.is_dma) and dep.eng == ins.eng:
                if ins.eng == "pe":
                    continue
                if kind == "WAR":
                    continue
            deps.append(dep)
        if ins.eng in self.pending:
            for x in self.pending.pop(ins.eng):
                if x.eng == ins.eng and (not x.is_dma) and (not ins.is_dma):
                    continue
                if x not in d:
                    deps.append(x)
        best = {}
        rest = []
        for dep in deps:
            if dep.is_dma:
                rest.append(dep)
            else:
                b = best.get(dep.eng)
                if b is None or dep.seq > b.seq:
                    best[dep.eng] = dep
        ins.deps = rest + list(best.values())
        ins.seq = len(self.order)
        if ins.is_dma:
            self.last_d[dma_key] = ins
        else:
            self.last_c[ins.eng] = ins
        if ins.is_dma:
            self.dma_cum[dma_key] = self.dma_cum.get(dma_key, 0) + 16
            ins.tick = self.dma_cum[dma_key]
            if out:
                self.out_keys.add(dma_key)
        self.order.append(ins)
        return ins

    def finalize(self):
        for ins in self.order:
            for dep in ins.deps:
                if not dep.is_dma:
                    dep.sig = True
        cnt = {e: 0 for e in self.COMPUTE}
        for ins in self.order:
            if (not ins.is_dma) and ins.sig:
                cnt[ins.eng] += 1
                ins.tick = cnt[ins.eng]
        cum = {}
        waited = {}
        per_eng = {}
        for ins in self.order:
            w = {}
            for dep in ins.deps:
                if dep.is_dma:
                    k = ("dma", dep.key)
                    v = cum[dep.key]
                else:
                    k = ("eng", dep.eng)
                    v = dep.tick
                if v > w.get(k, 0):
                    w[k] = v
            if ins.is_dma:
                cum[ins.key] = ins.tick
            wl = []
            we = waited.setdefault(ins.eng, {})
            for k, v in w.items():
                if we.get(k, 0) >= v:
                    continue
                we[k] = v
                wl.append((k, v))
            ins.waits = wl
            per_eng.setdefault(ins.eng, []).append(ins)
        self.per_eng = per_eng
        self.counts = cnt

    def run(self, sems_needed=None):
        nc = self.nc
        self.finalize()
        import contextlib

        with contextlib.ExitStack() as st:
            esem = {e: st.enter_context(nc.semaphore("sem_" + e)) for e in self.COMPUTE}
            dsem = {k: st.enter_context(nc.semaphore("dsem_%d" % i)) for i, k in enumerate(self.dma_cum)}
            block = st.enter_context(nc.Block())

            def emit_all(eng_name, eng):
                for ins in self.per_eng.get(eng_name, []):
                    for (k, v) in ins.waits:
                        s = dsem[k[1]] if k[0] == "dma" else esem[k[1]]
                        eng.wait_ge(s, v)
                    r = ins.emit(eng)
                    if ins.is_dma:
                        r.then_inc(dsem[ins.key], 16)
                    elif ins.sig:
                        r.then_inc(esem[ins.eng], 1)
                if eng_name == "sp":
                    for k in self.out_keys:
                        eng.wait_ge(dsem[k], self.dma_cum[k])

            @block.sync
            def _(e):
                emit_all("sp", e)

            @block.tensor
            def _(e):
                emit_all("pe", e)

            @block.scalar
            def _(e):
                emit_all("act", e)

            @block.vector
            def _(e):
                emit_all("dve", e)

            @block.gpsimd
            def _(e):
                emit_all("pool", e)


import contextlib
import math

NT = 2080
NPR = 2048
TCS = [(0, 512), (512, 512), (1024, 512), (1536, 512), (2048, 32)]
EPS = 1e-6
DIL = (1, 4, 16)
WIN = (128, 512, 2048)
NEG = -30000.0
O1, O2, O3, O4, O5, O6 = 1024, 2048, 3584, 5120, 6656, 7168
NPAR = 90
LL = 385
LT = 16


def _t5_bucket(dist):
    dist = np.asarray(dist, np.int64)
    df = np.maximum(dist, 1).astype(np.float32)
    large = 16 + (np.log(df / np.float32(16)) / np.float32(math.log(2048 / 16)) * np.float32(16)).astype(np.int32)
    return np.where(dist < 16, dist, np.minimum(large, 31))


def _cmat():
    C = np.zeros((33, 3, LL + LT), np.float32)
    for g, d in enumerate(DIL):
        for i in range(LL):
            if 128 <= i <= 256:
                C[int(_t5_bucket((i - 128) * d)), g, i] = 1.0
            else:
                C[32, g, i] = NEG
        for i in range(LT):
            dl = i - 8
            if dl >= 0 and dl % d == 0:
                C[int(_t5_bucket(dl)), g, LL + i] = 1.0
            else:
                C[32, g, LL + i] = NEG
    return C


import os
KSTOP = int(os.environ.get("KSTOP", "99"))


def build_program():
    nc = bass.Bass("TRN2", target_bir_lowering=False)
    P = Prog(nc)
    D = {}

    def din(name, shape):
        D[name] = nc.dram_tensor(name, list(shape), F32, kind="ExternalInput")

    def dout(name, shape):
        D[name] = nc.dram_tensor(name, list(shape), F32, kind="ExternalOutput")

    din("xT", (128, 8, NT)); din("xtok", (NT, 1024))
    din("wv", (2, 3, 128, 8, 256)); din("wqk", (4, 7, 128, 8, 128)); din("wlru", (8, 2, 128, 8, 128))
    din("wa", (128, 8, 128)); din("wx", (128, 8, 128)); din("wfin", (8, 128, 28, 128)); din("wout", (128, 8, 1024))
    din("par", (128, NPAR)); din("relb", (33, 12)); din("cmat", (33, 3, LL + LT)); din("ident", (128, 128))
    din("c128", (4, 128, 1024)); din("c512", (4, 512, 1024)); din("c2048", (4, 2048, 1024))
    din("sconv", (128, 8, 4, 3)); din("sh", (128, 8, 4))
    dout("ytok", (NT, 1024)); dout("kvp128", (128, 1024)); dout("kvp512", (512, 1024)); dout("kvp2048", (2048, 1024))
    dout("small", (128, 8, 20))
    dout("kvs128", (4, 128, 1024)); dout("kvs512", (4, 512, 1024)); dout("kvs2048", (4, 2048, 1024))
    LLT = LL + LT
    scr = nc.dram_tensor("scr", [12, 128, LLT], F32, kind="Internal")
    scr1 = nc.dram_tensor("scr1", [12, LLT], F32, kind="Internal")
    caches = [D["c128"], D["c512"], D["c2048"]]
    kvs = [D["kvs128"], D["kvs512"], D["kvs2048"]]
    kvp = [D["kvp128"], D["kvp512"], D["kvp2048"]]

    ST = contextlib.ExitStack()

    def sb(name, shape, dt=F32, stack=None):
        return (stack or ST).enter_context(nc.sbuf_tensor("s_" + name, list(shape), dt))

    ps = [ST.enter_context(nc.psum_tensor("ps%d" % i, [128, 512], F32)) for i in range(8)]
    bank_ctr = [0]

    reserved = set()

    def bank(reserve=False):
        while True:
            b = bank_ctr[0] % 8
            bank_ctr[0] += 1
            if b not in reserved:
                break
        if reserve:
            reserved.add(b)
        return b

    def PB(b):
        return ("ps", b)

    ctr = {}

    def rot(name, n):
        v = ctr.get(name, 0)
        ctr[name] = v + 1
        return v % n

    def mm(out, lhsT, rhs, start, stop, reads, writes):
        P.add("pe", lambda e: e.matmul(out, lhsT=lhsT, rhs=rhs, start=start, stop=stop, skip_group_check=True),
              reads=reads, writes=writes)

    def act(out, in_, func, reads, writes, scale=1.0, bias=0.0):
        P.add("act", lambda e: e.activation(out=out, in_=in_, func=func, scale=scale, bias=bias), reads=reads, writes=writes)

    def dve_tt(out, in0, in1, op, reads, writes):
        P.add("dve", lambda e: e.tensor_tensor(out=out, in0=in0, in1=in1, op=op), reads=reads, writes=writes)

    def dve_stt(out, in0, scalar, in1, op0, op1, reads, writes):
        P.add("dve", lambda e: e.scalar_tensor_tensor(out=out, in0=in0, scalar=scalar, in1=in1, op0=op0, op1=op1),
              reads=reads, writes=writes)

    def dve_ts(out, in0, s1, s2, op0, op1, reads, writes):
        if s2 is None:
            P.add("dve", lambda e: e.tensor_scalar(out=out, in0=in0, scalar1=s1, scalar2=None, op0=op0), reads=reads, writes=writes)
        else:
            P.add("dve", lambda e: e.tensor_scalar(out=out, in0=in0, scalar1=s1, scalar2=s2, op0=op0, op1=op1),
                  reads=reads, writes=writes)

    def dve_cp(out, in_, reads, writes):
        P.add("dve", lambda e: e.tensor_copy(out=out, in_=in_), reads=reads, writes=writes)

    def dve_rcp(out, in_, reads, writes):
        P.add("dve", lambda e: e.reciprocal(out=out, in_=in_), reads=reads, writes=writes)

    def dma(eng, out, in_, reads, writes, key, is_out=False):
        P.add(eng, lambda e: e.dma_start(out=out, in_=in_), reads=reads, writes=writes, dma_key=key, out=is_out)

    def bcast_mid(ap2d, n):
        a = ap2d.ap
        return bass.AP(ap2d.tensor, ap2d.offset, [list(a[0]), [0, n]] + [list(x) for x in a[1:]])

    ident_f = sb("ident_f", (128, 128)); ident_b = sb("ident_b", (128, 128), BF16); ones_b = sb("ones_b", (128, 128), BF16)
    par = sb("par", (128, NPAR)); der = sb("der", (128, 32))
    uT = sb("uT", (128, 8, NT), BF16)
    obg = sb("obg", (128, 4, NT), BF16)
    small = sb("small", (128, 8, 20))
    scv = sb("scv", (128, 8, 4, 3)); shh = sb("shh", (128, 8, 4))
    ring = [sb("ring%d" % i, (128, 8, 128), BF16) for i in range(6)]
    qTs = sb("qTs", (128, 12, 32), BF16); sgbS = sb("sgbS", (128, 4, 32), F32); BTps = sb("BTps", (128, 12, 8), F32)
    KVmK = [sb("KVmK%d" % i, (128, 512), BF16) for i in range(5)]
    KVmV = [sb("KVmV%d" % i, (128, 512), BF16) for i in range(9)]
    kTm = [sb("kTm%d" % i, (128, 512), BF16) for i in range(3)]
    Pfs = [sb("Pfs%d" % i, (128, 32), F32) for i in range(3)]
    PTs = [sb("PTs%d" % i, (128, 32), BF16) for i in range(3)]
    rdens = sb("rdens", (128, 128), F32); otmps = sb("otmps", (128, 128), F32)

    def ring_load(src_ap):
        s = rot("ring", 6)
        dma("pool", ring[s][:], src_ap, [], [("ring", s)], ("ring", s))
        return s

    cc_pieces = []
    for g in range(3):
        L = WIN[g]
        npc = 4 if g == 2 else 1
        rows = (L - 8) // npc
        for b in range(4):
            for pc in range(npc):
                n = rows * 1024
                o_dst = b * L * 1024 + pc * n
                o_src = b * L * 1024 + 8 * 1024 + pc * n
                cc_pieces.append((g, b, o_dst, o_src, n))
    P.nosbuf_keys.add("ccopy")

    def cc_next():
        if not cc_pieces:
            return
        g, b, o_dst, o_src, n = cc_pieces.pop()
        dst = bass.AP(kvs[g], o_dst, [[n // 16, 16], [1, n // 16]])
        src = bass.AP(caches[g], o_src, [[n // 16, 16], [1, n // 16]])
        dma("sp", dst, src, [], [("kvs", g, b, o_dst)], "ccopy", is_out=True)

    dma("sp", ident_f[:], D["ident"].ap(), [], ["ident_f"], "c0")
    dma("sp", par[:], D["par"].ap(), [], ["par"], "c1")
    dma("sp", scv[:], D["sconv"].ap(), [], ["scv"], "c2")
    dma("sp", shh[:], D["sh"].ap(), [], ["shh"], "c3")
    dve_cp(ident_b[:], ident_f[:], ["ident_f"], ["ident_b"])
    P.add("dve", lambda e: e.memset(ones_b[:], 1.0), writes=["ones_b"])
    dve_ts(der[:, 0:1], par[:, 88:89], float(128 ** -0.5), None, ALU.mult, None, ["par"], [("der", 0)])
    act(der[:, 17:25], par[:, 64:72], AF.Exp, ["par"], [("der", 2)], scale=-1.0)
    act(der[:, 17:25], der[:, 17:25], AF.Ln, [("der", 2)], [("der", 2)], bias=1.0)
    dve_ts(der[:, 1:9], der[:, 17:25], -8.0, None, ALU.mult, None, [("der", 2)], [("der", 1)])
    dve_ts(der[:, 9:17], der[:, 17:25], -16.0, None, ALU.mult, None, [("der", 2)], [("der", 1, 2)])
    DER = [("der", 0), ("der", 1), ("der", 1, 2)]

    SO = contextlib.ExitStack()
    BTc = sb("BTc", (128, 12, 128), F32, SO); BTp = sb("BTp", (128, 12, 128), F32, SO); BTt = sb("BTt", (8, 12, 8), F32, SO)
    wvb = [sb("wvb%d" % i, (128, 8, 256), BF16, SO) for i in range(2)]
    SA = contextlib.ExitStack()
    lines = sb("lines", (12, 3, LL + LT), F32, SA)
    relb = sb("relb", (33, 12), F32, SA); cm = sb("cm", (33, 3, LL + LT), F32, SA)
    wv_pre = {}
    for g in range(2):
        s_ = rot("wvb", 2)
        dma("pool", wvb[s_][:], D["wv"].ap()[0, g], [], [("wvb", s_)], ("wvb", s_))
        wv_pre[(0, g)] = s_
    dma("sp", relb[:], D["relb"].ap(), [], ["relb"], "c4")
    dma("sp", cm[:], D["cmat"].ap(), [], ["cm"], "c5")
    xT = sb("xT", (128, 8, NT), F32, SA)
    for kc in range(8):
        dma("sp", xT[:, kc, :], D["xT"].ap()[:, kc, :], [], [("xT", kc)], ("xT", kc))
    for g in range(3):
        b = 5 + g
        mm(ps[b][0:12, 0:LL + LT], relb[:], cm[:, g, :], True, True, ["relb", "cm"], [PB(b)])
        act(lines[:, g, :], ps[b][0:12, 0:LL + LT], AF.Identity, [PB(b)], [("lines", g)])
    for g in range(3):
        dma("sp", scr1.ap()[4 * g:4 * g + 4, :], lines[4 * g:4 * g + 4, g, :], [("lines", g)], [("scr1", g)], ("c6", g))
    dma("sp", scr.ap(), bass.AP(scr1, 0, [[LLT, 12], [0, 128], [1, LLT]]), [("scr1", g) for g in range(3)], ["scr"], "c7")
    dma("sp", BTc[:], bass.AP(scr, 128, [[LLT - 1, 128], [128 * LLT, 12], [1, 128]]), ["scr"], ["BTc"], "c8")
    dma("sp", BTp[:], bass.AP(scr, 256, [[LLT - 1, 128], [128 * LLT, 12], [1, 128]]), ["scr"], ["BTp"], "c9")
    dma("sp", BTt[:], bass.AP(scr, LL + 8, [[LLT - 1, 8], [128 * LLT, 12], [1, 8]]), ["scr"], ["BTt"], "c10")
    for k_ in ("c7", "c8", "c9", "c10"):
        P.nosbuf_keys.add(k_)

    with SA:
        sq = [sb("sq%d" % i, (128, NT), BF16, SA) for i in range(2)]
        sd = sb("sd", (128, NT), F32, SA)
        rstd = sb("rstd", (128, NT), F32, SA)
        for kc in range(8):
            s = kc % 2
            act(sq[s][:], xT[:, kc, :], AF.Square, [("xT", kc)], [("sq", s)])
            for ti, (t0, n) in enumerate(TCS):
                mm(ps[ti][:, 0:n], ones_b[:], sq[s][:, t0:t0 + n], kc == 0, kc == 7, ["ones_b", ("sq", s)], [PB(ti)])
        for ti, (t0, n) in enumerate(TCS):
            act(sd[:, t0:t0 + n], ps[ti][:, 0:n], AF.Ln, [PB(ti)], [("sd", ti)], scale=1.0 / 1024, bias=EPS)
            act(rstd[:, t0:t0 + n], sd[:, t0:t0 + n], AF.Exp, [("sd", ti)], [("rstd", ti)], scale=-0.5)
        for kc in range(8):
            dve_stt(uT[:, kc, :], xT[:, kc, :], par[:, kc:kc + 1], rstd[:], ALU.mult, ALU.mult,
                    [("xT", kc), "par"] + [("rstd", ti) for ti in range(5)], [("uT", kc)])
    bank_ctr[0] = 0
    UT = [("uT", kc) for kc in range(8)]
    if KSTOP <= 1:
        P.run(); return nc
    P.barrier()

    with contextlib.ExitStack() as SB:
        Vb = sb("Vb", (128, 3, 16, 256), BF16, SB)
        vst = [sb("vst%d" % i, (128, 256), F32, SB) for i in range(4)]
        Vsb = sb("Vsb", (8, 4, 3, 512), BF16, SB); Vsf = [sb("Vsf%d" % i, (8, 256), F32, SB) for i in range(2)]
        kss = [sb("kss%d" % i, (32, 128), F32, SB) for i in range(2)]
        qT = [sb("qT%d" % g, (128, NPR), BF16, SB) for g in range(3)]
        kT = [sb("kT%d" % g, (128, NPR), BF16, SB) for g in range(3)]
        kTs = sb("kTs", (128, 12, 32), BF16, SB)
        sgb = sb("sgb", (128, NPR), F32, SB)
        sqb = [KVmV[3], KVmV[4], KVmV[5]]
        rsq = [sb("rsq%d" % i, (128, 512), F32, SB) for i in range(3)]
        kf = [sb("kf%d" % i, (128, 512), F32, SB) for i in range(3)]
        kst = [sb("kst%d" % i, (128, 4, 128), F32, SB) for i in range(3)]
        Pf = [sb("Pf%d" % i, (128, 512), F32, SB) for i in range(3)]
        PT = [KVmV[0], KVmV[1], KVmV[2]]
        rden = sb("rden", (128, 512), F32, SB); otmp = sb("otmp", (128, 512), F32, SB)

        def tok_slice(g, i):
            if g == 0:
                return slice(128 * i, 128 * i + 128, 1)
            if g == 1:
                r, n = i // 4, i % 4
                s0 = r + 512 * n
                return slice(s0, s0 + 4 * 128, 4)
            return slice(i, i + 16 * 128, 16)

        def cm_view(t2d, g, c):
            d = DIL[g]
            if d == 1:
                return t2d[:, 512 * c:512 * c + 512]
            v = t2d[:, :].rearrange("p (r m) -> p m r", r=d)
            return v[:, (512 // d) * c:(512 // d) * (c + 1), :]

        def nat_view(ap2d, g):
            d = DIL[g]
            if d == 1:
                return ap2d
            return ap2d.rearrange("p (m r) -> p m r", r=d)

        for hp in range(2):
            KV = int(os.environ.get("KV", "99"))
            for g in range(3 if KV > 3 else 1):
                if (hp, g) in wv_pre:
                    s = wv_pre[(hp, g)]
                else:
                    s = rot("wvb", 2)
                    dma("pool", wvb[s][:], D["wv"].ap()[hp, g], [], [("wvb", s)], ("wvb", s))
                for i in range(16 if KV > 1 else 1):
                    b = bank()
                    sl = tok_slice(g, i)
                    for kc in range(8):
                        mm(ps[b][:, 0:256], uT[:, kc, sl], wvb[s][:, kc, :], kc == 0, kc == 7, [("uT", kc), ("wvb", s)], [PB(b)])
                    act(Vb[:, g, i, :], ps[b][:, 0:256], AF.Copy, [PB(b)], [("Vb", g, i)])
                    inwin = (g == 0 and i == 15) or (g == 1 and i % 4 == 3) or g == 2
                    if inwin and KV > 1 and not os.environ.get('NOSTORE'):
                        v = rot("vst", 4)
                        dve_cp(vst[v][:], ps[b][:, 0:256], [PB(b)] + ([("Vb", g, i)] if os.environ.get("SERIAL") else []), [("vst", v)])
                        c0 = 512 + hp * 256
                        if g == 0:
                            dst = kvp[0].ap()[0:128, c0:c0 + 256]
                        elif g == 1:
                            dst = kvp[1].ap()[(i // 4)::4, c0:c0 + 256]
                        else:
                            dst = kvp[2].ap()[i::16, c0:c0 + 256]
                        if not os.environ.get("NODMA"):
                            dma("sp", dst, vst[v][:], [("vst", v)], [], ("vst", v), is_out=True)
                for bb in range(4 if KV > 2 else 0):
                    b = bank()
                    for kc in range(8):
                        mm(ps[b][0:8, 0:256], uT[:, kc, NPR + 8 * bb:NPR + 8 * bb + 8], wvb[s][:, kc, :], kc == 0, kc == 7,
                           [("uT", kc), ("wvb", s)], [PB(b)])
                    act(Vsb[:, bb, g, hp * 256:hp * 256 + 256], ps[b][0:8, 0:256], AF.Copy, [PB(b)], [("Vsb", bb, g, hp)])
                    vs_ = rot("Vsf", 2)
                    dve_cp(Vsf[vs_][:], ps[b][0:8, 0:256], [PB(b)], [("Vsf", vs_)])
                    dma("sp", kvs[g].ap()[bb, WIN[g] - 8:WIN[g], 512 + hp * 256:512 + hp * 256 + 256], Vsf[vs_][:], [("Vsf", vs_)],
                        [("kvsn2", g, bb, hp)], ("Vsf", vs_), is_out=True)

            if KSTOP <= 3:
                P.run(); return nc
            for hl in range(2):
                h = 2 * hp + hl
                tasks = [(j, ti) for j in range(7) for ti in range(5)]
                slots = {}

                def qk_s1(task, h=h, slots=slots):
                    j, ti = task
                    t0, n = TCS[ti]
                    if ti == 0:
                        slots[j] = ring_load(D["wqk"].ap()[h, j])
                        cc_next()
                    s = slots[j]
                    A = bank()
                    for kc in range(8):
                        mm(ps[A][:, 0:n], ring[s][:, kc, :], uT[:, kc, t0:t0 + n], kc == 0, kc == 7, [("ring", s), ("uT", kc)], [PB(A)])
                    if j == 6:
                        if ti < 4:
                            act(sgb[:, t0:t0 + n], ps[A][:, 0:n], AF.Silu, [PB(A)], [("sgb", ti)])
                        else:
                            act(sgbS[:, h, :], ps[A][:, 0:n], AF.Silu, [PB(A)], [("sgbS", h)])
                        return None
                    x = rot("sqb", 3)
                    act(sqb[x][:, 0:n], ps[A][:, 0:n], AF.Square, [PB(A)], [("sqb", x)])
                    return (A, x)

                def qk_s2(task, st, h=h):
                    if st is None:
                        return
                    j, ti = task
                    t0, n = TCS[ti]
                    A, x = st
                    Bk = bank()
                    mm(ps[Bk][:, 0:n], ones_b[:], sqb[x][:, 0:n], True, True, ["ones_b", ("sqb", x)], [PB(Bk)])
                    act(rsq[x][:, 0:n], ps[Bk][:, 0:n], AF.Ln, [PB(Bk)], [("rsq", x)], scale=1.0 / 128, bias=EPS)
                    act(rsq[x][:, 0:n], rsq[x][:, 0:n], AF.Exp, [("rsq", x)], [("rsq", x)], scale=-0.5)
                    g = j % 3
                    gh = g * 4 + h
                    if j < 3:
                        if ti < 4:
                            dve_stt(cm_view(qT[g], g, ti), nat_view(ps[A][:, 0:512], g), der[:, 0:1], nat_view(rsq[x][:, 0:512], g),
                                    ALU.mult, ALU.mult, [PB(A), ("rsq", x)] + DER, [("qT", g, ti)])
                        else:
                            dve_stt(qTs[:, gh, :], ps[A][:, 0:n], der[:, 0:1], rsq[x][:, 0:n], ALU.mult, ALU.mult,
                                    [PB(A), ("rsq", x)] + DER, [("qTs", gh)])
                        return
                    tiles = [i for i in range(4) if (g == 2 or (g == 1 and ti == 3) or (g == 0 and ti == 3 and i == 3))] if ti < 4 else []
                    if ti < 4 and not tiles:
                        dve_stt(cm_view(kT[g], g, ti), nat_view(ps[A][:, 0:512], g), par[:, 89:90], nat_view(rsq[x][:, 0:512], g),
                                ALU.mult, ALU.mult, [PB(A), ("rsq", x), "par"], [("kT", g, ti)])
                        return None
                    y = rot("kf", 3)
                    dve_stt(kf[y][:, 0:n], ps[A][:, 0:n], par[:, 89:90], rsq[x][:, 0:n], ALU.mult, ALU.mult,
                            [PB(A), ("rsq", x), "par"], [("kf", y)])
                    if ti < 4:
                        dve_cp(cm_view(kT[g], g, ti), nat_view(kf[y][:, 0:512], g), [("kf", y)], [("kT", g, ti)])

                        def s3(tiles=tiles, y=y, g=g, ti=ti):
                            C = bank()
                            for i in tiles:
                                P.add("pe", (lambda e, C=C, i=i, y=y: e.transpose(ps[C][:, 128 * i:128 * i + 128],
                                                                               kf[y][:, 128 * i:128 * i + 128], ident_f[:])),
                                      reads=[("kf", y), "ident_f"], writes=[PB(C)])
                            z = rot("kst", 3)
                            i0, ni = tiles[0], len(tiles)
                            act(kst[z][:, i0:i0 + ni, :], ps[C][:, 128 * i0:128 * (i0 + ni)].rearrange("p (i d) -> p i d", d=128),
                                AF.Copy, [PB(C)], [("kst", z)])
                            tok0 = 512 * ti + 128 * i0 - (NPR - WIN[g])
                            dst = bass.AP(kvp[g], tok0 * 1024 + h * 128, [[1024, 128], [128 * 1024, ni], [1, 128]])
                            dma("sp", dst, kst[z][:, i0:i0 + ni, :], [("kst", z)], [], ("kst", z), is_out=True)
                        return s3
                    else:
                        act(kTs[:, gh, :], kf[y][:, 0:n], AF.Copy, [("kf", y)], [("kTs", gh)])

                        def s3s(y=y, g=g):
                            C = bank()
                            P.add("pe", (lambda e, C=C, y=y: e.transpose(ps[C][0:32, 0:128], kf[y][:, 0:32], ident_f[:])),
                                  reads=[("kf", y), "ident_f"], writes=[PB(C)])
                            z = rot("kss", 2)
                            act(kss[z][:], ps[C][0:32, 0:128], AF.Copy, [PB(C)], [("kss", z)])
                            L = WIN[g]
                            for bb in range(4):
                                dma("sp", kvs[g].ap()[bb, L - 8:L, h * 128:h * 128 + 128], kss[z][8 * bb:8 * bb + 8, :], [("kss", z)],
                                    [("kvsn", g, h, bb)], ("kss", z), is_out=True)
                        return s3s

                prev1 = None
                prev2 = None
                for task in tasks:
                    st = qk_s1(task)
                    if prev2 is not None:
                        prev2()
                        prev2 = None
                    if prev1 is not None:
                        prev2 = qk_s2(*prev1)
                    prev1 = (task, st)
                if prev2 is not None:
                    prev2()
                prev2 = qk_s2(*prev1)
                if prev2 is not None:
                    prev2()

                if KSTOP <= 4:
                    P.run(); return nc
                for c in range(4):
                    Ob = bank(True); Db = bank(True)
                    fl = {"o": True, "d": True}
                    QK_R = lambda g: [("kT", g, t) for t in range(4)] + [("qT", g, t) for t in range(4)]

                    def mk_unit(g, prev, c=c, h=h, hl=hl, Ob=Ob, Db=Db, fl=fl):
                        gh = g * 4 + h
                        u = {}
                        if g < 2:
                            blocks = []
                            for i in range(4):
                                if g == 0:
                                    n_ = 4 * c + i
                                    if prev and n_ == 0:
                                        continue
                                    qs = slice(128 * n_, 128 * n_ + 128)
                                    ks = slice(128 * (n_ - prev), 128 * (n_ - prev) + 128)
                                    vt = n_ - prev
                                else:
                                    qs = slice(i * 512 + 128 * c, i * 512 + 128 * c + 128)
                                    ks = slice(i * 512 + 128 * (c - prev), i * 512 + 128 * (c - prev) + 128)
                                    vt = i * 4 + c - prev
                                blocks.append((i, vt, qs, ks))
                            i0 = blocks[0][0]
                            nb = len(blocks)

                            def S(u=u):
                                Sb = bank()
                                u["Sb"] = Sb
                                for (i, vt, qs, ks) in blocks:
                                    mm(ps[Sb][:, 128 * i:128 * i + 128], kT[g][:, ks], qT[g][:, qs], True, True, QK_R(g), [PB(Sb)])

                            def soft(u=u):
                                Sb = u["Sb"]
                                x = rot("Pf", 3)
                                u["x"] = x
                                BT = BTp if prev else BTc
                                dve_tt(Pf[x][:, 128 * i0:512].rearrange("p (i q) -> p i q", q=128),
                                       ps[Sb][:, 128 * i0:512].rearrange("p (i q) -> p i q", q=128),
                                       bcast_mid(BT[:, gh, :], nb), ALU.add, [PB(Sb), "BTc", "BTp"], [("Pf", x)])
                                act(PT[x][:, 128 * i0:512], Pf[x][:, 128 * i0:512], AF.Exp, [("Pf", x)], [("PT", x)])

                            def pvf(u=u):
                                x = u["x"]
                                for (i, vt, qs, ks) in blocks:
                                    oc = ps[Ob][:, 128 * i:128 * i + 128] if g == 0 else ps[Ob][:, i::4]
                                    mm(oc, Vb[:, g, vt, hl * 128:hl * 128 + 128], PT[x][:, 128 * i:128 * i + 128], fl["o"], False,
                                       [("PT", x), ("Vb", g, vt)], [PB(Ob)])
                                    fl["o"] = False
                                if g == 0:
                                    mm(ps[Db][:, 128 * i0:512], ones_b[:], PT[x][:, 128 * i0:512], fl["d"], False, [("PT", x), "ones_b"], [PB(Db)])
                                    fl["d"] = False
                                else:
                                    for (i, vt, qs, ks) in blocks:
                                        mm(ps[Db][:, i::4], ones_b[:], PT[x][:, 128 * i:128 * i + 128], fl["d"], False, [("PT", x), "ones_b"], [PB(Db)])
                                        fl["d"] = False
                        else:
                            KK = 32 * (c + 1)

                            def S(u=u):
                                Sb = bank()
                                u["Sb"] = Sb
                                for r in range(16):
                                    mm(ps[Sb][0:KK, 32 * r:32 * r + 32], kT[2][:, r * 128:r * 128 + KK],
                                       qT[2][:, r * 128 + 32 * c:r * 128 + 32 * c + 32], True, True, QK_R(2), [PB(Sb)])

                            def soft(u=u):
                                Sb = u["Sb"]
                                x = rot("Pf", 3)
                                u["x"] = x
                                dve_tt(Pf[x][0:KK, :].rearrange("p (r q) -> p r q", q=32), ps[Sb][0:KK, :].rearrange("p (r q) -> p r q", q=32),
                                       bcast_mid(BTc[0:KK, gh, 32 * c:32 * c + 32], 16), ALU.add, [PB(Sb), "BTc"], [("Pf", x)])
                                act(PT[x][0:KK, :], Pf[x][0:KK, :], AF.Exp, [("Pf", x)], [("PT", x)])

                            def pvf(u=u):
                                x = u["x"]
                                for r in range(16):
                                    mm(ps[Ob][:, r::16], Vb[0:KK, 2, r, hl * 128:hl * 128 + 128], PT[x][0:KK, 32 * r:32 * r + 32], fl["o"], False,
                                       [("PT", x), ("Vb", 2, r)], [PB(Ob)])
                                    fl["o"] = False
                                for r in range(16):
                                    mm(ps[Db][:, r::16], ones_b[0:KK, :], PT[x][0:KK, 32 * r:32 * r + 32], fl["d"], False, [("PT", x), "ones_b"], [PB(Db)])
                                    fl["d"] = False
                        u["S"], u["soft"], u["pv"] = S, soft, pvf
                        return u

                    units = [mk_unit(0, 0), mk_unit(0, 1), mk_unit(1, 0)] + ([mk_unit(1, 1)] if c > 0 else []) + [mk_unit(2, 0)]
                    units[0]["S"](); units[1]["S"]()
                    for k, u in enumerate(units):
                        u["soft"]()
                        if k + 2 < len(units):
                            units[k + 2]["S"]()
                        u["pv"]()
                    act(rden[:], ps[Db][:, :], AF.Ln, [PB(Db)], ["rden"])
                    act(rden[:], rden[:], AF.Exp, ["rden"], ["rden"], scale=-1.0)
                    dve_tt(otmp[:], rden[:], sgb[:, 512 * c:512 * c + 512], ALU.mult, ["rden", ("sgb", c)], ["otmp"])
                    dve_tt(obg[:, h, 512 * c:512 * c + 512], ps[Ob][:, :], otmp[:], ALU.mult, [PB(Ob), "otmp"], [("obg", h, c)])
                    reserved.discard(Ob); reserved.discard(Db)

        if KSTOP <= 5:
            P.run(); return nc
        dve_cp(BTps[:], BTp[:, :, 0:8], ["BTp"], ["BTps"])
        Os = bank(True)
        sfirst = {"o": True}

        def spv(cols, vap, ptap, K, reads):
            first = sfirst["o"]
            sfirst["o"] = False
            mm(cols(0), vap, ptap, first, False, reads, [PB(Os)])
            mm(cols(128), ones_b[0:K, :], ptap, False, False, reads + ["ones_b"], [PB(Os)])

        QS = [("qTs", i) for i in range(12)]
        KS = [("kTs", i) for i in range(12)]
        for bb in range(4):
            Sb = bank()
            for gh in range(12):
                mm(ps[Sb][0:8, 8 * gh:8 * gh + 8], kTs[:, gh, 8 * bb:8 * bb + 8], qTs[:, gh, 8 * bb:8 * bb + 8], True, True, QS + KS, [PB(Sb)])
            x = rot("Pf", 3)
            dve_tt(Pf[x][0:8, 0:96], ps[Sb][0:8, 0:96], BTt[:, :, :].rearrange("p a b -> p (a b)"), ALU.add, [PB(Sb), "BTt"], [("Pf", x)])
            act(PT[x][0:8, 0:96], Pf[x][0:8, 0:96], AF.Exp, [("Pf", x)], [("PT", x)])
            for gh in range(12):
                g, h = gh // 4, gh % 4
                spv((lambda off, h=h, bb=bb: ps[Os][:, off + h * 32 + 8 * bb:off + h * 32 + 8 * bb + 8]),
                    Vsb[:, bb, g, h * 128:h * 128 + 128], PT[x][0:8, 8 * gh:8 * gh + 8], 8,
                    [("PT", x), ("Vsb", bb, g, 0), ("Vsb", bb, g, 1)])
        stiles = [(bb, g, r) for bb in range(4) for g in range(3) for r in range((1, 4, 8)[g])]

        def sm_L(t):
            bb, g, r = stiles[t]
            kk = rot("KVmK", 5)
            kv = rot("KVmV", 9)
            dma("pool", KVmK[kk][:], caches[g].ap()[bb, r::DIL[g], 0:512], [], [("KVmK", kk)], ("KVmK", kk))
            dma("pool", KVmV[kv][:], caches[g].ap()[bb, r::DIL[g], 512:1024], [], [("KVmV", kv)], ("KVmV", kv))
            us[t] = {"kk": kk, "kv": kv}

        def sm_T(t):
            u = us[t]
            kk = u["kk"]
            Tb = bank()
            psb = ps[Tb][:, 0:256].bitcast(BF16)
            for h in range(4):
                P.add("pe", (lambda e, psb=psb, kk=kk, h=h: e.transpose(psb[:, 128 * h:128 * h + 128],
                                                                       KVmK[kk][:, 128 * h:128 * h + 128], ident_b[:])),
                      reads=[("KVmK", kk), "ident_b"], writes=[PB(Tb)])
            km = rot("kTm", 3)
            dve_cp(kTm[km][:], psb, [PB(Tb)], [("kTm", km)])
            u["km"] = km

        def sm_S(t):
            bb, g, r = stiles[t]
            u = us[t]
            nq = (8, 2, 1)[g]
            km = u["km"]
            Sb = bank()
            qsl = slice(8 * bb, 8 * bb + 8) if g == 0 else (slice(8 * bb + r, 8 * bb + r + 5, 4) if g == 1 else slice(8 * bb + r, 8 * bb + r + 1))
            for h in range(4):
                mm(ps[Sb][:, 8 * h:8 * h + nq], kTm[km][:, 128 * h:128 * h + 128], qTs[:, g * 4 + h, qsl], True, True,
                   [("kTm", km)] + QS, [PB(Sb)])
            x = rot("Pfs", 3)
            if g == 2:
                for h in range(4):
                    act(PTs[x][:, 8 * h:8 * h + 1], ps[Sb][:, 8 * h:8 * h + 1], AF.Exp, [PB(Sb), "BTps"],
                        [("PTs", x, h)] + ([("PTs", x)] if h == 3 else []), bias=BTps[:, 8 + h, 0:1])
            else:
                dve_tt(Pfs[x][:, :].rearrange("p (h a) -> p h a", a=8)[:, :, 0:nq],
                       ps[Sb][:, 0:32].rearrange("p (h a) -> p h a", a=8)[:, :, 0:nq],
                       BTps[:, 4 * g:4 * g + 4, 0:nq], ALU.add, [PB(Sb), "BTps"], [("Pfs", x)])
                act(PTs[x][:, :].rearrange("p (h a) -> p h a", a=8)[:, :, 0:nq],
                    Pfs[x][:, :].rearrange("p (h a) -> p h a", a=8)[:, :, 0:nq], AF.Exp, [("Pfs", x)],
                    [("PTs", x)] + [("PTs", x, h) for h in range(4)])
            u["x"], u["qsl"], u["nq"] = x, qsl, nq

        def sm_PV(t):
            u = us[t]
            kv, x, qsl, nq = u["kv"], u["x"], u["qsl"], u["nq"]
            for h in range(4):
                def cols(off, h=h, qsl=qsl):
                    return ps[Os][:, slice(off + h * 32 + qsl.start, off + h * 32 + qsl.stop, qsl.step)]
                spv(cols, KVmV[kv][:, 128 * h:128 * h + 128], PTs[x][:, 8 * h:8 * h + nq], 128,
                    [("PTs", x), ("PTs", x, h), ("KVmV", kv)])

        us = {}
        nt_ = len(stiles)
        sm_k = [0]

        def sm_step():
            k = sm_k[0]
            sm_k[0] += 1
            if k >= nt_ + 7:
                return False
            if 0 <= k - 7 < nt_:
                sm_PV(k - 7)
            if 0 <= k - 5 < nt_:
                sm_S(k - 5)
            if 0 <= k - 3 < nt_:
                sm_T(k - 3)
            if k < nt_:
                sm_L(k)
            return True

        def sm_finish():
            while sm_step():
                pass
            act(rdens[:], ps[Os][:, 128:256], AF.Ln, [PB(Os)], ["rdens"])
            act(rdens[:], rdens[:], AF.Exp, ["rdens"], ["rdens"], scale=-1.0)
            dve_tt(otmps[:], rdens[:], sgbS[:, :, :].rearrange("p a b -> p (a b)"), ALU.mult,
                   ["rdens"] + [("sgbS", hh) for hh in range(4)], ["otmps"])
            dve_tt(obg[:, :, NPR:NT], ps[Os][:, 0:128].rearrange("p (a b) -> p a b", b=32), otmps[:].rearrange("p (a b) -> p a b", b=32),
                   ALU.mult, [PB(Os), "otmps"], [("obg", "s")])
            reserved.discard(Os)
    SO.close()
    OBG = [("obg", h, c) for h in range(4) for c in range(4)] + [("obg", "s")]
    if KSTOP <= 6:
        P.run(); return nc
    P.barrier()

    hsg = sb("hsg", (128, 8, NT), BF16)
    while cc_pieces:
        cc_next()
    with contextlib.ExitStack() as SC:
        wab = sb("wab", (128, 8, 128), BF16, SC); wxb = sb("wxb", (128, 8, 128), BF16, SC)
        dma("pool", wab[:], D["wa"].ap(), [], ["wab"], "wab")
        dma("pool", wxb[:], D["wx"].ap(), [], ["wxb"], "wxb")
        xaf = [sb("xaf%d" % i, (128, 3 + NPR), F32, SC) for i in range(2)]
        xs = [sb("xs%d" % i, (128, 4, 11), F32, SC) for i in range(2)]
        xc = [sb("xc%d" % i, (128, NT), F32, SC) for i in range(2)]
        xcb = [sb("xcb%d" % i, (128, NT), BF16, SC) for i in range(2)]
        t1 = sb("t1", (128, NT), F32, SC); af = sb("af", (128, NT), F32, SC); iff = sb("iff", (128, NT), F32, SC)
        hs = sb("hs", (128, NT), F32, SC); sga = sb("sga", (128, NT), F32, SC)
        for q in range(2):
            P.add("dve", (lambda e, q=q: e.memset(xaf[q][:, 0:3], 0.0)), writes=[("xaf0", q)])
        HB = [(0, 1024), (1024, NT)]
        HP = [(0, 1024), (1024, NPR)]
        HT = [[0, 1], [2, 3, 4]]
        T1h = [[("t1", ti) for ti in HT[h_]] for h_ in range(2)]
        IFh = [[("iff", ti) for ti in HT[h_]] for h_ in range(2)]
        SGh = [[("sga", ti) for ti in HT[h_]] for h_ in range(2)]
        HSS = [("hss", bb) for bb in range(4)]
        HS = [("hsp", 0), ("hsp", 1)] + HSS
        slot2 = {}

        def XAh(q, h_):
            return ([("xaf", q, 0), ("xaf", q, 1), ("xaf0", q)] if h_ == 0 else [("xaf", q, 1), ("xaf", q, 2), ("xaf", q, 3)])

        def lru_s1a(kc):
            q = kc % 2
            s = ring_load(D["wlru"].ap()[kc, 0])
            slot2[kc] = ring_load(D["wlru"].ap()[kc, 1])
            act(xs[q][:, :, 0:3], scv[:, kc, :, :], AF.Copy, ["scv"], [("xs", q)])
            for ti, (t0, n) in enumerate(TCS):
                A = bank()
                for k2 in range(8):
                    mm(ps[A][:, 0:n], ring[s][:, k2, :], uT[:, k2, t0:t0 + n], k2 == 0, k2 == 7, [("ring", s), ("uT", k2)], [PB(A)])
                if ti < 4:
                    act(xaf[q][:, 3 + t0:3 + t0 + n], ps[A][:, 0:n], AF.Copy, [PB(A)], [("xaf", q, ti)])
                else:
                    act(xs[q][:, :, 3:11], ps[A][:, 0:32].rearrange("p (b t) -> p b t", t=8), AF.Copy, [PB(A)], [("xs2", q)])
                if ti in (1, 3):
                    h_ = ti // 2
                    c0, c1 = HP[h_]
                    act(xc[q][:, c0:c1], xaf[q][:, 3 + c0:3 + c1], AF.Identity, XAh(q, h_) + ["par"], [("xcp", q, h_)],
                        scale=par[:, 8 + 3 * 8 + kc:8 + 3 * 8 + kc + 1], bias=par[:, 40 + kc:41 + kc])

        def lru_s1b(kc):
            q = kc % 2
            XA = [("xaf", q, ti) for ti in range(4)] + [("xaf0", q)]
            XS = [("xs", q), ("xs2", q)]
            w = lambda j: par[:, 8 + j * 8 + kc:8 + j * 8 + kc + 1]
            cb = par[:, 40 + kc:41 + kc]
            for h_ in range(2):
                c0, c1 = HP[h_]
                for j in range(3):
                    dve_stt(xc[q][:, c0:c1], xaf[q][:, j + c0:j + c1], w(j), xc[q][:, c0:c1], ALU.mult, ALU.add,
                            XAh(q, h_) + ["par", ("xcp", q, h_)], [("xcp", q, h_)])
                if h_ == 0:
                    dve_cp(xcb[q][:, 0:1024], xc[q][:, 0:1024], [("xcp", q, 0)], [("xcb", q, 0)])
            xcs = xc[q][:, NPR:NT].rearrange("p (b t) -> p b t", t=8)
            dve_ts(xcs, xs[q][:, :, 3:11], w(3), cb, ALU.mult, ALU.add, XS + ["par"], [("xcs", q)])
            for j in range(3):
                dve_stt(xcs, xs[q][:, :, j:j + 8], w(j), xcs, ALU.mult, ALU.add, XS + ["par", ("xcs", q)], [("xcs", q)])
            dve_cp(xcb[q][:, 1024:NT], xc[q][:, 1024:NT], [("xcp", q, 1), ("xcs", q)], [("xcb", q, 1)])
            dve_cp(small[:, kc, 0:3], xaf[q][:, NPR:NPR + 3], XA, [("small", kc, 0)])
            dve_cp(small[:, kc, 4:16].rearrange("p (b t) -> p b t", t=3), xs[q][:, :, 8:11], XS, [("small", kc, 1)])

        def lru_s2a(kc):
            q = kc % 2
            for ti, (t0, n) in enumerate(TCS):
                xb_ = ("xcb", q, 0 if ti < 2 else 1)
                A = bank()
                mm(ps[A][:, 0:n], wab[:, kc, :], xcb[q][:, t0:t0 + n], True, True, ["wab", xb_], [PB(A)])
                act(t1[:, t0:t0 + n], ps[A][:, 0:n], AF.Sigmoid, [PB(A), "par"], [("t1", ti)], bias=par[:, 48 + kc:49 + kc])
                Bk = bank()
                mm(ps[Bk][:, 0:n], wxb[:, kc, :], xcb[q][:, t0:t0 + n], True, True, ["wxb", xb_], [PB(Bk)])
                act(iff[:, t0:t0 + n], ps[Bk][:, 0:n], AF.Sigmoid, [PB(Bk), "par"], [("iff", ti)], bias=par[:, 56 + kc:57 + kc])
            for h_ in range(2):
                c0, c1 = HB[h_]
                act(af[:, c0:c1], t1[:, c0:c1], AF.Exp, T1h[h_] + DER, [("af", h_)], scale=der[:, 1 + kc:2 + kc])
            for h_ in range(2):
                c0, c1 = HB[h_]
                act(t1[:, c0:c1], t1[:, c0:c1], AF.Exp, T1h[h_] + DER, T1h[h_], scale=der[:, 9 + kc:10 + kc])
            for h_ in range(2):
                c0, c1 = HB[h_]
                act(t1[:, c0:c1], t1[:, c0:c1], AF.Sqrt, T1h[h_], T1h[h_], scale=-1.0, bias=1.0)

        def lru_s2b(kc):
            q = kc % 2
            for h_ in range(2):
                c0, c1 = HB[h_]
                p0, p1 = HP[h_]
                xr = [("xcp", q, h_)] + ([("xcs", q)] if h_ == 1 else [])
                dve_tt(iff[:, c0:c1], iff[:, c0:c1], xc[q][:, c0:c1], ALU.mult, IFh[h_] + xr, IFh[h_])
                dve_tt(iff[:, c0:c1], iff[:, c0:c1], t1[:, c0:c1], ALU.mult, IFh[h_] + T1h[h_], IFh[h_])
                if h_ == 0:
                    P.add("dve", lambda e: e.tensor_tensor_scan(out=hs[:, 0:1024], data0=af[:, 0:1024], data1=iff[:, 0:1024], initial=0.0,
                                                                op0=ALU.mult, op1=ALU.add), reads=[("af", 0)] + IFh[0], writes=[("hsp", 0)])
                else:
                    P.add("dve", lambda e: e.tensor_tensor_scan(out=hs[:, 1024:NPR], data0=af[:, 1024:NPR], data1=iff[:, 1024:NPR],
                                                                initial=hs[:, 1023:1024], op0=ALU.mult, op1=ALU.add),
                          reads=[("af", 1), ("hsp", 0)] + IFh[1], writes=[("hsp", 1)])
            for bb in range(4):
                c0 = NPR + 8 * bb
                P.add("dve", (lambda e, c0=c0, bb=bb, kc=kc: e.tensor_tensor_scan(out=hs[:, c0:c0 + 8], data0=af[:, c0:c0 + 8],
                                                                                data1=iff[:, c0:c0 + 8], initial=shh[:, kc, bb:bb + 1],
                                                                                op0=ALU.mult, op1=ALU.add)),
                      reads=[("af", 1), "shh"] + IFh[1], writes=[("hss", bb)])
            dve_cp(small[:, kc, 3:4], hs[:, NPR - 1:NPR], HS, [("small", kc, 2)])
            dve_cp(small[:, kc, 16:20], hs[:, NPR + 7:NT:8], HS, [("small", kc, 3)])

        def lru_s2c(kc):
            s2 = slot2[kc]
            for ti, (t0, n) in enumerate(TCS):
                A = bank()
                for k2 in range(8):
                    mm(ps[A][:, 0:n], ring[s2][:, k2, :], uT[:, k2, t0:t0 + n], k2 == 0, k2 == 7, [("ring", s2), ("uT", k2)], [PB(A)])
                act(sga[:, t0:t0 + n], ps[A][:, 0:n], AF.Silu, [PB(A)], [("sga", ti)])

        def lru_s2d(kc):
            dve_tt(hsg[:, kc, 0:1024], hs[:, 0:1024], sga[:, 0:1024], ALU.mult, [("hsp", 0)] + SGh[0], [("hsg", kc)])
            dve_tt(hsg[:, kc, 1024:NT], hs[:, 1024:NT], sga[:, 1024:NT], ALU.mult, [("hsp", 1)] + HSS + SGh[1], [("hsg", kc)])

        sm_step(); sm_step()
        lru_s1a(0)
        sm_step()
        lru_s1b(0)
        for kc in range(8):
            sm_step()
            if kc + 1 < 8:
                lru_s1a(kc + 1)
            sm_step()
            lru_s2a(kc)
            sm_step()
            if kc + 1 < 8:
                lru_s1b(kc + 1)
            sm_step()
            lru_s2c(kc)
            sm_step()
            lru_s2b(kc)
            sm_step()
            lru_s2d(kc)
            sm_step()
        sm_finish()
        dma("sp", D["small"].ap(), small[:], [("small", kc, i) for kc in range(8) for i in range(4)], [], "small", is_out=True)
    if KSTOP <= 7:
        P.run(); return nc
    P.barrier()

    merged = sb("merged", (128, 8, NT), BF16)
    wo = sb("wo", (128, 8, 1024), BF16)
    dma("pool", wo[:], D["wout"].ap(), [], ["wo"], "wo")
    xk = [sb("xk%d" % i, (128, 1024), F32) for i in range(2)]
    xk_pre = {}
    for tt in range(2):
        s_ = rot("xk", 2)
        dma("sp", xk[s_][0:128, :], D["xtok"].ap()[128 * tt:128 * tt + 128, :], [], [("xk", s_)], ("xk", s_))
        xk_pre[tt] = s_
    with contextlib.ExitStack() as SD:
        wf = [sb("wf%d" % i, (128, 28, 128), BF16, SD) for i in range(2)]
        s1 = [sb("s1_%d" % i, (128, 512), F32, SD) for i in range(2)]
        s2t = [sb("s2_%d" % i, (128, 512), F32, SD) for i in range(2)]
        m1 = [sb("m1_%d" % i, (128, 512), F32, SD) for i in range(2)]
        m2 = [sb("m2_%d" % i, (128, 512), F32, SD) for i in range(2)]
        for dc in range(8):
            s = rot("wf", 2)
            dma("pool", wf[s][:], D["wfin"].ap()[dc], [], [("wf", s)], ("wf", s))
            for ti, (t0, n) in enumerate(TCS):
                Ya = bank(); Yb = bank(); G1 = bank(); G2 = bank()
                for kc in range(8):
                    mm(ps[Ya][:, 0:n], wf[s][:, kc, :], hsg[:, kc, t0:t0 + n], kc == 0, kc == 7, [("wf", s), ("hsg", kc)], [PB(Ya)])
                for k4 in range(4):
                    mm(ps[Yb][:, 0:n], wf[s][:, 8 + k4, :], obg[:, k4, t0:t0 + n], k4 == 0, k4 == 3, [("wf", s)] + OBG, [PB(Yb)])
                for kc in range(8):
                    mm(ps[G1][:, 0:n], wf[s][:, 12 + kc, :], uT[:, kc, t0:t0 + n], kc == 0, kc == 7, [("wf", s), ("uT", kc)], [PB(G1)])
                for kc in range(8):
                    mm(ps[G2][:, 0:n], wf[s][:, 20 + kc, :], uT[:, kc, t0:t0 + n], kc == 0, kc == 7, [("wf", s), ("uT", kc)], [PB(G2)])
                x = rot("mg", 2)
                act(s1[x][:, 0:n], ps[G1][:, 0:n], AF.Sigmoid, [PB(G1), "par"], [("s1", x)], bias=par[:, 72 + dc:73 + dc])
                act(s2t[x][:, 0:n], ps[G2][:, 0:n], AF.Sigmoid, [PB(G2), "par"], [("s2", x)], bias=par[:, 80 + dc:81 + dc])
                dve_tt(m1[x][:, 0:n], ps[Ya][:, 0:n], s1[x][:, 0:n], ALU.mult, [PB(Ya), ("s1", x)], [("m1", x)])
                dve_tt(m2[x][:, 0:n], ps[Yb][:, 0:n], s2t[x][:, 0:n], ALU.mult, [PB(Yb), ("s2", x)], [("m2", x)])
                dve_tt(merged[:, dc, t0:t0 + n], m1[x][:, 0:n], m2[x][:, 0:n], ALU.add, [("m1", x), ("m2", x)], [("merged", dc)])
    MG = [("merged", dc) for dc in range(8)]
    if KSTOP <= 8:
        P.run(); return nc
    P.barrier()

    with contextlib.ExitStack() as SE:
        xk = xk + [sb("xk%d" % i, (128, 1024), F32, SE) for i in (2, 3)]
        yst = [sb("yst%d" % i, (128, 1024), F32, SE) for i in range(4)]
        for tt in range(17):
            r0 = 128 * tt
            nt = 128 if tt < 16 else 32
            if tt in xk_pre:
                s = xk_pre[tt]
            else:
                s = rot("xk", 4)
                dma("sp", xk[s][0:nt, :], D["xtok"].ap()[r0:r0 + nt, :], [], [("xk", s)], ("xk", s))
            y = rot("yst", 4)
            for half in range(2):
                A = bank()
                for kc in range(8):
                    mm(ps[A][0:nt, :], merged[:, kc, r0:r0 + nt], wo[:, kc, 512 * half:512 * half + 512], kc == 0, kc == 7, MG + ["wo"], [PB(A)])
                dve_tt(yst[y][0:nt, 512 * half:512 * half + 512], ps[A][0:nt, :], xk[s][0:nt, 512 * half:512 * half + 512], ALU.add,
                       [PB(A), ("xk", s)], [("yst", y, half)])
            dma("sp", D["ytok"].ap()[r0:r0 + nt, :], yst[y][0:nt, :], [("yst", y, 0), ("yst", y, 1)], [], ("yst", y), is_out=True)

    P.run()
    ST.close()
    return nc


def _fm(W):
    k = W.shape[0] // 128
    return np.ascontiguousarray(W.reshape(k, 128, W.shape[1]).transpose(1, 0, 2))


_CACHE = {}
_PREP_ONLY = [False]


def kernel(x_prompt, x_sample, cache_kv_w128, cache_kv_w512, cache_kv_w2048, state_conv, state_h,
           g_norm, w_in, b_merge, conv_w, conv_b, lru_w_a, lru_b_a, lru_w_x, lru_b_x, lru_lambda,
           g_q, g_k, rel_bias, w_lru_proj, w_attn_proj, w_out):
    f = lambda a: np.asarray(a, np.float32)
    x_prompt, x_sample = f(x_prompt), f(x_sample)
    W = f(w_in)[0]
    wv = np.stack([np.stack([_fm(W[:, O4 + (g * 4 + 2 * hp) * 128:O4 + (g * 4 + 2 * hp) * 128 + 256]) for g in range(3)]) for hp in range(2)])
    def qkcols(h, j):
        if j < 3:
            return O2 + (j * 4 + h) * 128
        if j < 6:
            return O3 + ((j - 3) * 4 + h) * 128
        return O5 + h * 128
    wqk = np.stack([np.stack([_fm(W[:, qkcols(h, j):qkcols(h, j) + 128]) for j in range(7)]) for h in range(4)])
    wlru = np.stack([np.stack([_fm(W[:, kc * 128:kc * 128 + 128]), _fm(W[:, O1 + kc * 128:O1 + kc * 128 + 128])]) for kc in range(8)])
    wa = np.ascontiguousarray(f(lru_w_a)[0].transpose(1, 0, 2))
    wx = np.ascontiguousarray(f(lru_w_x)[0].transpose(1, 0, 2))
    wlp, wap, wo_ = f(w_lru_proj)[0], f(w_attn_proj)[0], f(w_out)[0]
    wfin = np.stack([np.concatenate([_fm(wlp[:, dc * 128:dc * 128 + 128]), _fm(wap[:, dc * 128:dc * 128 + 128]),
                                     _fm(W[:, O6 + dc * 128:O6 + dc * 128 + 128]),
                                     _fm(W[:, O6 + 1024 + dc * 128:O6 + 1024 + dc * 128 + 128])], axis=1) for dc in range(8)])
    wout = _fm(wo_)
    par = np.zeros((128, NPAR), np.float32)
    v8 = lambda v: np.asarray(v, np.float32).reshape(-1, 128).T
    par[:, 0:8] = v8(f(g_norm)[0])
    cw = f(conv_w)[0]
    for j in range(4):
        par[:, 8 + 8 * j:16 + 8 * j] = v8(cw[j])
    par[:, 40:48] = v8(f(conv_b)[0]); par[:, 48:56] = v8(f(lru_b_a)[0]); par[:, 56:64] = v8(f(lru_b_x)[0])
    par[:, 64:72] = v8(f(lru_lambda)[0]); par[:, 72:88] = v8(f(b_merge)[0])
    par[:, 88] = f(g_q)[0]; par[:, 89] = f(g_k)[0]
    relb = np.concatenate([f(rel_bias), np.ones((1, 12), np.float32)], axis=0)
    cmat = _cmat()
    ident = np.eye(128, dtype=np.float32)
    c128, c512, c2048 = f(cache_kv_w128)[0], f(cache_kv_w512)[0], f(cache_kv_w2048)[0]
    sc_, sh_ = f(state_conv)[0], f(state_h)[0]

    in_maps = []
    for c in range(8):
        xs_ = x_sample[4 * c:4 * c + 4].reshape(32, 1024)
        xtok = np.ascontiguousarray(np.concatenate([x_prompt[c], xs_], axis=0))
        xT = np.ascontiguousarray(xtok.T.reshape(8, 128, NT).transpose(1, 0, 2))
        scc = np.ascontiguousarray(sc_[4 * c:4 * c + 4].reshape(4, 3, 8, 128).transpose(3, 2, 0, 1))
        shc = np.ascontiguousarray(sh_[4 * c:4 * c + 4].reshape(4, 8, 128).transpose(2, 1, 0))
        in_maps.append({
            "xT": xT, "xtok": xtok, "wv": wv, "wqk": wqk, "wlru": wlru, "wa": wa, "wx": wx, "wfin": wfin, "wout": wout,
            "par": par, "relb": relb, "cmat": cmat, "ident": ident,
            "c128": np.ascontiguousarray(c128[4 * c:4 * c + 4].reshape(4, 128, 1024)),
            "c512": np.ascontiguousarray(c512[4 * c:4 * c + 4].reshape(4, 512, 1024)),
            "c2048": np.ascontiguousarray(c2048[4 * c:4 * c + 4].reshape(4, 2048, 1024)),
            "sconv": scc, "sh": shc,
        })
    if _PREP_ONLY[0]:
        return in_maps
    if "nc" not in _CACHE:
        _CACHE["nc"] = build_program()
    nc = _CACHE["nc"]
    ncores = int(os.environ.get("KCORES", "8"))
    res = run_bass_kernel_spmd(nc, in_maps[:ncores], core_ids=list(range(ncores)))
    R = list(res.results)
    while len(R) < 8:
        R.append({k: np.zeros_like(v) for k, v in R[0].items()})
    y_prompt = np.stack([R[c]["ytok"][0:2048] for c in range(8)])
    y_sample = np.concatenate([R[c]["ytok"][2048:].reshape(4, 8, 1024) for c in range(8)], axis=0)
    kvp_o = [np.stack([R[c][nm].reshape(-1, 2, 4, 128) for c in range(8)])[None] for nm in ("kvp128", "kvp512", "kvp2048")]
    kvs_o = [np.concatenate([R[c][nm].reshape(4, -1, 2, 4, 128) for c in range(8)], axis=0)[None] for nm in ("kvs128", "kvs512", "kvs2048")]
    sm = np.stack([R[c]["small"] for c in range(8)])
    smf = sm.transpose(0, 3, 2, 1).reshape(8, 20, 1024)
    conv_prompt = smf[:, 0:3][None]
    h_prompt = smf[:, 3][None]
    conv_sample = smf[:, 4:16].reshape(8, 4, 3, 1024).reshape(32, 3, 1024)[None]
    h_sample = smf[:, 16:20].reshape(32, 1024)[None]
    asf = lambda a: np.ascontiguousarray(a, dtype=np.float32)
    return (asf(y_prompt), asf(y_sample), asf(kvp_o[0]), asf(kvp_o[1]), asf(kvp_o[2]), asf(conv_prompt), asf(h_prompt),
            asf(kvs_o[0]), asf(kvs_o[1]), asf(kvs_o[2]), asf(conv_sample), asf(h_sample))
```
